# Optimizing a Trainium2 kernel written in Bass

```python
import jax, jax.numpy as jnp
from jax import lax
import numpy as np

D_MODEL = 2048
BATCH = 2
SEQ = 8192
DEPTH = 2

HEAD_DIM = 128
MIX_WIDTH = D_MODEL
N_HEADS_A = MIX_WIDTH // (2 * HEAD_DIM)
N_HEADS_B = MIX_WIDTH // (2 * HEAD_DIM)
N_HEADS_C = MIX_WIDTH // HEAD_DIM
N_KV_C = N_HEADS_C // 4
GRID_W = 64
NA_ROW_WIN = 8
NA_COL_WIN = 16
DILATED_BRANCHES = ((128, 1), (512, 4), (2048, 16))
C_RADIUS = 128
BAND_BLOCK = 128
D_FF = 4 * D_MODEL
NORM_EPS = 1e-6
NEG_INF = -1e30

kernel_name = "hybrid_natten_dilated_swa_encoder"


def rms_norm(x, g):
    xf = x.astype(jnp.float32)
    y = xf * lax.rsqrt(jnp.mean(xf * xf, axis=-1, keepdims=True) + NORM_EPS)
    return (y * g.astype(jnp.float32)).astype(x.dtype)


def split_heads(x, n_heads):
    b, t, _ = x.shape
    return x.reshape(b, t, n_heads, HEAD_DIM).transpose(0, 2, 1, 3)


def merge_heads(x):
    b, h, t, d = x.shape
    return x.transpose(0, 2, 1, 3).reshape(b, t, h * d)


def alibi_slopes(n_heads):
    return 2.0 ** (-8.0 * jnp.arange(1, n_heads + 1, dtype=jnp.float32) / n_heads)


def neighbourhood_attention(q, k, v, rpb):
    b, h, t, hd = q.shape
    rows = t // GRID_W
    kr = min(NA_ROW_WIN, rows)
    qg = q.reshape(b, h, rows, GRID_W, hd)
    kg = k.reshape(b, h, rows, GRID_W, hd)
    vg = v.reshape(b, h, rows, GRID_W, hd)
    col = jnp.arange(GRID_W)
    col_start = jnp.clip(col - NA_COL_WIN // 2, 0, GRID_W - NA_COL_WIN)
    col_idx = col_start[:, None] + jnp.arange(NA_COL_WIN)[None, :]
    dc = col_idx - col[:, None] + (NA_COL_WIN - 1)
    rpb_c = rpb.astype(jnp.float32)[:, :, dc]
    scale = HEAD_DIM ** -0.5

    def row_fn(r):
        r0 = jnp.clip(r - kr // 2, 0, rows - kr)
        q_r = lax.dynamic_index_in_dim(qg, r, axis=2, keepdims=False)
        k_w = lax.dynamic_slice_in_dim(kg, r0, kr, axis=2)[:, :, :, col_idx]
        v_w = lax.dynamic_slice_in_dim(vg, r0, kr, axis=2)[:, :, :, col_idx]
        s = jnp.einsum('bhcd,bhrckd->bhcrk', q_r, k_w).astype(jnp.float32) * scale
        dr = r0 + jnp.arange(kr) - r + (NA_ROW_WIN - 1)
        bias = jnp.take(rpb_c, dr, axis=1).transpose(0, 2, 1, 3)
        s = s + bias[None]
        p = jax.nn.softmax(s.reshape(b, h, GRID_W, kr * NA_COL_WIN), axis=-1).reshape(s.shape)
        return jnp.einsum('bhcrk,bhrckd->bhcd', p.astype(v.dtype), v_w)

    out = lax.map(row_fn, jnp.arange(rows))
    return out.transpose(1, 2, 0, 3, 4).reshape(b, h, t, hd)


def banded_attention(q, k, v, radius, dist_scale, slopes, sink=None, with_lse=False):
    n, hq, seq_len, hd = q.shape
    hkv = k.shape[1]
    grp = hq // hkv
    block = min(BAND_BLOCK, seq_len)
    n_blk = -(-seq_len // block)
    lp = n_blk * block
    span = block + 2 * radius
    qp = jnp.pad(q, ((0, 0), (0, 0), (0, lp - seq_len), (0, 0))).reshape(n, hkv, grp, lp, hd)
    pad_k = ((0, 0), (0, 0), (radius, radius + lp - seq_len), (0, 0))
    kp = jnp.pad(k, pad_k)
    vp = jnp.pad(v, pad_k)
    dist = jnp.abs(jnp.arange(block)[:, None] - jnp.arange(span)[None, :] + radius)
    band = dist <= radius
    bias = (-slopes[:, None, None] * (dist * dist_scale).astype(jnp.float32)[None]).reshape(hkv, grp, block, span)
    scale = hd ** -0.5
    if sink is not None:
        sk = sink.astype(jnp.float32).reshape(1, hkv, grp, 1)

    def block_fn(bi):
        s0 = bi * block
        qb = lax.dynamic_slice_in_dim(qp, s0, block, axis=3)
        kb = lax.dynamic_slice_in_dim(kp, s0, span, axis=2)
        vb = lax.dynamic_slice_in_dim(vp, s0, span, axis=2)
        kpos = s0 - radius + jnp.arange(span)
        valid = band & ((kpos >= 0) & (kpos < seq_len))[None, :]
        s = jnp.einsum('nkgqd,nksd->nkgqs', qb, kb).astype(jnp.float32) * scale + bias
        s = jnp.where(valid, s, NEG_INF)
        m = jnp.max(s, axis=-1)
        if sink is not None:
            m = jnp.maximum(m, sk)
        e = jnp.exp(s - m[..., None])
        den = jnp.sum(e, axis=-1)
        if sink is not None:
            den = den + jnp.exp(sk - m)
        o = jnp.einsum('nkgqs,nksd->nkgqd', (e / den[..., None]).astype(v.dtype), vb)
        if with_lse:
            return o, m + jnp.log(den)
        return o

    res = lax.map(block_fn, jnp.arange(n_blk))
    if with_lse:
        o, lse = res
        lse = lse.transpose(1, 2, 3, 0, 4).reshape(n, hq, lp)[:, :, :seq_len]
    else:
        o = res
    o = o.transpose(1, 2, 3, 0, 4, 5).reshape(n, hq, lp, hd)[:, :, :seq_len]
    if with_lse:
        return o, lse
    return o


def dilated_attention(q, k, v):
    b, h, t, hd = q.shape
    slopes = alibi_slopes(h)
    outs, lses = [], []
    for window, dil in DILATED_BRANCHES:
        sub = t // dil

        def to_res(z):
            return z.reshape(b, h, sub, dil, hd).transpose(0, 3, 1, 2, 4).reshape(b * dil, h, sub, hd)

        o, lse = banded_attention(to_res(q), to_res(k), to_res(v), window // (2 * dil), dil, slopes,
                                  with_lse=True)
        outs.append(o.reshape(b, dil, h, sub, hd).transpose(0, 2, 3, 1, 4).reshape(b, h, t, hd))
        lses.append(lse.reshape(b, dil, h, sub).transpose(0, 2, 3, 1).reshape(b, h, t))
    w = jax.nn.softmax(jnp.stack(lses, axis=0), axis=0)
    return jnp.sum(w[..., None].astype(q.dtype) * jnp.stack(outs, axis=0), axis=0)


def even_mixer(h, w_in, rpb, w_out):
    wa = N_HEADS_A * HEAD_DIM
    wb = N_HEADS_B * HEAD_DIM
    proj = h @ w_in
    qa, ka, va, qb, kb, vb = jnp.split(proj, [wa, 2 * wa, 3 * wa, 3 * wa + wb, 3 * wa + 2 * wb], axis=-1)
    oa = neighbourhood_attention(split_heads(qa, N_HEADS_A), split_heads(ka, N_HEADS_A),
                                 split_heads(va, N_HEADS_A), rpb)
    ob = dilated_attention(split_heads(qb, N_HEADS_B), split_heads(kb, N_HEADS_B),
                           split_heads(vb, N_HEADS_B))
    o = jnp.concatenate([merge_heads(oa), merge_heads(ob)], axis=-1)
    return o @ w_out


def odd_mixer(h, w_qkv, sink, w_out):
    wq = N_HEADS_C * HEAD_DIM
    wkv = N_KV_C * HEAD_DIM
    proj = h @ w_qkv
    q, k, v = jnp.split(proj, [wq, wq + wkv], axis=-1)
    o = banded_attention(split_heads(q, N_HEADS_C), split_heads(k, N_KV_C), split_heads(v, N_KV_C),
                         C_RADIUS, 1, alibi_slopes(N_HEADS_C), sink=sink)
    return merge_heads(o) @ w_out


def sq_relu_mlp(h, w1, w2):
    a = jax.nn.relu(h @ w1)
    return (a * a) @ w2


def setup_inputs(seed: int = 0) -> dict:
    key = jax.random.key(seed)
    ks = jax.random.split(key, 13)
    n_even = (DEPTH + 1) // 2
    n_odd = DEPTH // 2
    d = D_MODEL
    w_even_in = 3 * (N_HEADS_A + N_HEADS_B) * HEAD_DIM
    w_odd_in = (N_HEADS_C + 2 * N_KV_C) * HEAD_DIM
    f32 = jnp.float32
    return {
        'x': jax.random.normal(ks[0], (BATCH, SEQ, d), f32),
        'attn_norm': 1.0 + 0.02 * jax.random.normal(ks[1], (DEPTH, d), f32),
        'mlp_norm': 1.0 + 0.02 * jax.random.normal(ks[2], (DEPTH, d), f32),
        'w_mlp_in': jax.random.normal(ks[3], (DEPTH, d, D_FF), f32) * d ** -0.5,
        'w_mlp_out': jax.random.normal(ks[4], (DEPTH, D_FF, d), f32) * D_FF ** -0.5,
        'even_w_in': jax.random.normal(ks[5], (n_even, d, w_even_in), f32) * d ** -0.5,
        'even_rpb': 0.5 * jax.random.normal(ks[6], (n_even, N_HEADS_A, 2 * NA_ROW_WIN - 1, 2 * NA_COL_WIN - 1), f32),
        'even_w_out': jax.random.normal(ks[7], (n_even, MIX_WIDTH, d), f32) * MIX_WIDTH ** -0.5,
        'odd_w_qkv': jax.random.normal(ks[8], (n_odd, d, w_odd_in), f32) * d ** -0.5,
        'odd_sink': jax.random.normal(ks[9], (n_odd, N_HEADS_C), f32),
        'odd_w_out': jax.random.normal(ks[10], (n_odd, MIX_WIDTH, d), f32) * MIX_WIDTH ** -0.5,
        'final_norm': 1.0 + 0.02 * jax.random.normal(ks[11], (d,), f32),
    }


def reference(x, attn_norm, mlp_norm, w_mlp_in, w_mlp_out, even_w_in, even_rpb, even_w_out,
              odd_w_qkv, odd_sink, odd_w_out, final_norm):
    for i in range(DEPTH):
        h = rms_norm(x, attn_norm[i])
        j = i // 2
        if i % 2 == 0:
            x = x + even_mixer(h, even_w_in[j], even_rpb[j], even_w_out[j])
        else:
            x = x + odd_mixer(h, odd_w_qkv[j], odd_sink[j], odd_w_out[j])
        x = x + sq_relu_mlp(rms_norm(x, mlp_norm[i]), w_mlp_in[i], w_mlp_out[i])
    return rms_norm(x, final_norm)
```

```python
import numpy as np
from contextlib import ExitStack
import concourse.bass as bass
import concourse.mybir as mybir
from concourse.bass_utils import run_bass_kernel_spmd

F32 = mybir.dt.float32
BF16 = mybir.dt.bfloat16
AF = mybir.ActivationFunctionType
ALU = mybir.AluOpType

D = 2048
SEQ = 8192
DFF = 8192
NCORE = 8
OWN = 2048
NQ0 = 2304
EXT = 4352
QOFF = 1024
AOFF = 768
NA = 2816
BIG = 1e30
EPS = 1e-6
QSCALE = 128.0 ** -0.5
DEBUG = False
MODE = None
MLP_DBG = 3
MLP_G = 768
P1BANK = 6

B_VARIANT = {1: {0: 0, 1: 1, 16: 3, 17: 4}, 4: {0: 5, 1: 6, 2: 6, 3: 7, 4: 8}, 16: {0: 9, 1: 10}}
NVB = 11


def b_variant(d, b):
    if d == 1:
        return B_VARIANT[1].get(b, 2)
    return B_VARIANT[d][b]


def b_blocks(d):
    q0 = QOFF // d
    qn = NQ0 // d
    out = []
    b = 0
    while 128 * b < qn:
        out.append((b, q0 + 128 * b, min(128, qn - 128 * b)))
        b += 1
    return out


A_TYPE_P = {0: 8, 1: 1, 2: 2, 3: 15, 4: 16}
A_P_TYPE = {1: 1, 2: 2, 15: 3, 16: 4}


def host_tables(c, rpb):
    cq = c % 4
    T0 = cq * OWN
    q = np.arange(128)
    k = np.arange(128)
    biasA = np.full((8, 128, 5, 5, 128), -BIG, np.float32)
    for ty, p in A_TYPE_P.items():
        r = 32 * cq - 2 + 2 * p
        qr = q // 64
        qc = q % 64
        gq = r + qr
        for ch in range(5):
            kr = -4 + 2 * ch + k // 64
            kcol = k % 64
            cand = r + kr
            g = np.where(cand < 0, cand + 8, np.where(cand >= 128, cand - 8, cand))
            rel = kr[:, None] - qr[None, :]
            okr = (rel >= -4) & (rel <= 3)
            qreal = (gq >= 0) & (gq < 128)
            gk = np.where(qreal[None, :], g[:, None], cand[:, None])
            dr = gk - gq[None, :]
            okr = okr & (dr >= -7) & (dr <= 7)
            c0 = np.clip(qc - 8, 0, 48)
            okc = (kcol[:, None] >= c0[None, :]) & (kcol[:, None] < c0[None, :] + 16)
            dc = kcol[:, None] - qc[None, :]
            ok = okr & okc
            if ch == 4:
                ok = ok & (k[:, None] < 64)
            dri = np.clip(dr + 7, 0, 14)
            dci = np.clip(dc + 15, 0, 30)
            for h in range(8):
                vals = rpb[h][dri, dci]
                biasA[h, :, ty, ch, :] = np.where(ok, vals, -BIG)
    dmB = np.full((128, NVB, 2, 128), BIG, np.float32)
    for d in (1, 4, 16):
        for (b, qs, nq) in b_blocks(d):
            v = b_variant(d, b)
            i = np.arange(128)
            for ch in range(2):
                j = 128 * ch + k
                dist = np.abs(i[None, :] + 64 - j[:, None])
                gk = T0 - 1152 + d * (qs - 64 + j)
                gqq = T0 - 1152 + d * (qs + i)
                kval = (gk >= 0) & (gk < SEQ)
                qval = (gqq >= 0) & (gqq < SEQ)
                ok = (dist <= 64) & (kval[:, None] | ~qval[None, :])
                dmB[:, v, ch, :] = np.where(ok, (d * dist).astype(np.float32), BIG)
    dmC = np.full((128, 3, 3, 128), BIG, np.float32)
    for v, b in ((0, 0), (1, 8), (2, 15)):
        i = np.arange(128)
        for ch in range(3):
            j = 128 * ch + k
            dist = np.abs(128 + i[None, :] - j[:, None])
            gk = T0 - 128 + 128 * b + j
            kval = (gk >= 0) & (gk < SEQ)
            ok = (dist <= 128) & kval[:, None]
            dmC[:, v, ch, :] = np.where(ok, dist.astype(np.float32), BIG)
    return biasA, dmB, dmC


def host_x_ext(c, x):
    bi, cq = c // 4, c % 4
    T0 = cq * OWN
    xe = np.zeros((EXT, D), np.float32)
    g0 = T0 - 1152
    lo, hi = max(g0, 0), min(g0 + EXT, SEQ)
    xe[lo - g0:hi - g0] = x[bi, lo:hi]
    if cq == 0:
        xe[-256 - g0:0 - g0] = x[bi, 256:512]
    if cq == 3:
        xe[SEQ - g0:SEQ + 192 - g0] = x[bi, SEQ - 512:SEQ - 320]
    return xe


class Slot:
    def __init__(self, kk, name):
        self.kk = kk
        self.sem = kk.es.enter_context(kk.nc.semaphore(name))
        self.val = 0

    def dma(self, q, out, in_):
        self.kk.engs[q].dma_start(out=out, in_=in_).then_inc(self.sem, 16)
        self.val += 16
        return ('d', self, self.val)

    def tok(self):
        return ('d', self, self.val) if self.val else None


class KK:
    def __init__(self, nc, es):
        self.nc = nc
        self.es = es
        self.engs = {'pe': nc.tensor, 'act': nc.scalar, 'dve': nc.vector, 'pool': nc.gpsimd, 'sp': nc.sync}
        self.sem = {}
        self.cnt = {}
        for e in ('pe', 'act', 'dve', 'pool'):
            self.sem[e] = es.enter_context(nc.semaphore('m_' + e))
            self.cnt[e] = 0
        self.waited = {}
        self.slots = []
        self.nslot = 0
        self.pool = []
        self.phase_slots = []

    def slot(self, name=None, persistent=False):
        if not persistent and self.pool:
            s = self.pool.pop()
            self.phase_slots.append(s)
            return s
        self.nslot += 1
        s = Slot(self, name or ('sl%d' % self.nslot))
        self.slots.append(s)
        if not persistent:
            self.phase_slots.append(s)
        return s

    def recycle(self):
        self.pool.extend(self.phase_slots)
        self.phase_slots = []

    def mark(self, e, instr):
        instr.then_inc(self.sem[e], 1)
        self.cnt[e] += 1
        return ('c', e, self.cnt[e])

    def wait(self, e, *toks):
        for tok in toks:
            if tok is None:
                continue
            if isinstance(tok, list):
                self.wait(e, *tok)
                continue
            kind, key, val = tok
            if kind == 'c':
                semh, kid = self.sem[key], key
            else:
                semh, kid = key.sem, id(key)
            if self.waited.get((e, kid), 0) >= val:
                continue
            self.engs[e].wait_ge(semh, val)
            self.waited[(e, kid)] = val


def sl(start, n, d=1):
    return slice(start, start + d * (n - 1) + 1, d)


def build_program():
    nc = bass.Bass("TRN2", target_bir_lowering=False)
    es = ExitStack()
    kk = KK(nc, es)
    pe, act, dve, pool, sp = nc.tensor, nc.scalar, nc.vector, nc.gpsimd, nc.sync

    def din(name, shape, dt=F32):
        return nc.dram_tensor(name, list(shape), dt, kind="ExternalInput").ap()

    def dscr(name, shape, dt):
        return nc.dram_tensor(name, list(shape), dt, kind=("ExternalOutput" if DEBUG else "Internal")).ap()

    x_ext = din("x_ext", [EXT, D])
    gains = din("gains", [5, D])
    w_in0 = din("w_in0", [D, 6144])
    w_out0 = din("w_out0", [D, D])
    w_qkv1 = din("w_qkv1", [D, 3072])
    w_out1 = din("w_out1", [D, D])
    w1s = [din("w1_0", [D, DFF]), din("w1_1", [D, DFF])]
    w2s = [din("w2_0", [DFF, D]), din("w2_1", [DFF, D])]
    biasA_d = din("biasA", [8, 128, 25 * 128])
    dmB_d = din("dmB", [128, NVB * 2 * 128])
    dmC_d = din("dmC", [128, 9 * 128])
    sink_d = din("sink", [1, 16])
    ident_d = din("ident", [128, 128])
    y_out = nc.dram_tensor("y", [OWN, D], F32, kind="ExternalOutput").ap()

    QT0 = dscr("QT0", [2048, NQ0], BF16)
    KaT = dscr("KaT", [1024, NA], BF16)
    VaT = dscr("VaT", [1024, NA], BF16)
    KbT = dscr("KbT", [1024, EXT], BF16)
    VbT = dscr("VbT", [1024, EXT], BF16)
    OT0 = dscr("OT0", [2048, NQ0], BF16)
    X1a = dscr("X1a", [NQ0, D], F32)
    X1 = dscr("X1", [NQ0, D], F32)
    QT1 = dscr("QT1", [2048, OWN], BF16)
    KT1 = dscr("KT1", [512, NQ0], BF16)
    VT1 = dscr("VT1", [512, NQ0], BF16)
    OT1 = dscr("OT1", [2048, OWN], BF16)
    X2a = dscr("X2a", [OWN, D], F32)

    uniq = [0]

    def sb(name, shape, dt, stack=None):
        uniq[0] += 1
        return (stack or es).enter_context(nc.sbuf_tensor("%s_%d" % (name, uniq[0]), list(shape), dt))

    ident_f = sb("ident_f", [128, 128], F32)
    ident_b = sb("ident_b", [128, 128], BF16)
    ones_b = sb("ones_b", [128, 128], BF16)
    ones_f = sb("ones_f", [1, 128], F32)
    grep = sb("grep", [128, D], F32)
    grep2 = sb("grep2", [128, D], F32)
    dummy = sb("dmy_t", [128, 8], F32)
    small = sb("small", [128, 64], F32)
    pf = [es.enter_context(nc.psum_tensor("pf%d" % i, [128, 512], F32)) for i in range(8)]

    class _PB:
        def __init__(self, t):
            self.t = t

        def __getitem__(self, idx):
            return self.t[:, :].bitcast(BF16)[idx]

    pb = [_PB(pf[6]), _PB(pf[7])]

    class _Alias:
        def __getitem__(self, i):
            return pf_free[6 + i]

        def __setitem__(self, i, v):
            pf_free[6 + i] = v

    pf_free = [None] * 8
    pb_free = _Alias()

    c_slot = kk.slot("const", persistent=True)
    c_slot.dma('sp', ident_f[:], ident_d[:, :])
    t_c = c_slot.tok()
    kk.wait('dve', t_c)
    dve.tensor_copy(out=ident_b[:], in_=ident_f[:])
    dve.memset(ones_b[:], 1.0)
    dve.memset(dummy[:], 0.0)
    t_init = kk.mark('dve', dve.memset(ones_f[:], 1.0))

    def barrier():
        toks = []
        for s in kk.slots:
            kk.wait('sp', s.tok())
            kk.wait('pool', s.tok())
        kk.wait('pe', t_init, pf_free[0])
        toks.append(kk.mark('pe', pe.matmul(pf[0][0:8, 0:8], lhsT=ident_b[0:8, 0:8], rhs=ident_b[0:8, 0:8],
                                            start=True, stop=True)))
        toks.append(kk.mark('act', act.activation(out=dummy[:, 0:1], in_=dummy[:, 4:5], func=AF.Copy)))
        toks.append(kk.mark('dve', dve.tensor_copy(out=dummy[:, 1:2], in_=dummy[:, 5:6])))
        toks.append(kk.mark('pool', pool.tensor_copy(out=dummy[:, 2:3], in_=dummy[:, 6:7])))
        for e in ('pe', 'act', 'dve', 'pool', 'sp'):
            kk.wait(e, *toks)
        for i in range(8):
            pf_free[i] = None
        kk.recycle()

    g_slot = kk.slot("gain", persistent=True)
    g_slot2 = kk.slot("gain2", persistent=True)

    def load_gain(gi, dst=None):
        dst = grep if dst is None else dst
        return (g_slot if dst is grep else g_slot2).dma('sp', dst[:], gains[gi, :].partition_broadcast(128))

    def norm_tile(xt, t_x, hT, tcol, st, t_gain, extra_wait=None, xs_out=None):
        i = st['i']
        st['i'] += 1
        hb = st['hb'][i % 2]
        jk = st['junk']
        ss = small[:, 8 + (i % 4): 9 + (i % 4)]
        sd = small[:, 12 + (i % 4): 13 + (i % 4)]
        rs = small[:, 16 + (i % 4): 17 + (i % 4)]
        kk.wait('act', t_x, st.get('junk_free'), st['ss_free'][i % 4])
        t1 = kk.mark('act', act.activation(out=jk[:], in_=xt, func=AF.Square, accum_out=ss))
        kk.wait('act', t1)
        t2 = kk.mark('act', act.activation(out=sd, in_=ss, func=AF.Sqrt, bias=st['eps'][:, 0:1], scale=1.0 / D))
        kk.wait('dve', t2)
        t3 = kk.mark('dve', dve.reciprocal(out=rs, in_=sd))
        kk.wait('dve', t3, t_gain, st['hb_free'][i % 2], extra_wait)
        t4 = kk.mark('dve', dve.scalar_tensor_tensor(out=hb[:], in0=xt, scalar=rs, in1=grep[:], op0=ALU.mult, op1=ALU.mult))
        st['ss_free'][i % 4] = t4
        if xs_out is not None:
            xs_out.append(t3)
        kk.wait('pe', t4, pb_free[0], pb_free[1])
        tp = None
        for k in range(16):
            tp = pe.transpose(pb[k // 8][:, (k % 8) * 128:(k % 8 + 1) * 128], hb[:, k * 128:(k + 1) * 128], ident_b[:])
        tp = kk.mark('pe', tp)
        st['hb_free'][i % 2] = tp
        kk.wait('act', tp)
        kk.wait('dve', tp)
        e0 = kk.mark('act', act.activation(out=hT[:, 0:8, tcol:tcol + 128],
                                           in_=pb[0][:, :].rearrange("p (k t) -> p k t", k=8), func=AF.Copy))
        e1 = kk.mark('dve', dve.tensor_copy(out=hT[:, 8:16, tcol:tcol + 128],
                                            in_=pb[1][:, :].rearrange("p (k t) -> p k t", k=8)))
        pb_free[0] = e0
        pb_free[1] = e1
        return [e0, e1]

    def norm_state(stack):
        st = {'i': 0}
        st['hb'] = [sb("hb0", [128, D], BF16, stack), sb("hb1", [128, D], BF16, stack)]
        st['junk'] = sb("junk", [128, D], BF16, stack)
        st['eps'] = sb("epsb", [128, 1], F32, stack)
        st['ss_free'] = [None] * 4
        st['hb_free'] = [None, None]
        st['eps_tok'] = kk.mark('dve', dve.memset(st['eps'][:], EPS))
        kk.wait('act', st['eps_tok'])
        return st

    def proj_fm(hT, W, jobs, stack_parent, wname):
        with ExitStack() as st:
            NW = 3
            wb = [sb("%s_w%d" % (wname, i), [128, 16, 128], BF16, st) for i in range(NW)]
            wsl = [kk.slot() for _ in range(NW)]
            wfree = [None] * NW
            maxn = max(j[2] - j[1] for j in jobs)
            NS = 2
            stg = [sb("%s_s%d" % (wname, i), [128, maxn], BF16, st) for i in range(NS)]
            ssl = [kk.slot() for _ in range(NS)]
            Wv = W.rearrange("(k p) c -> p k c", p=128)
            wtok = {}

            def issue_w(ji):
                s = ji % NW
                kk.wait('pool', wfree[s])
                wtok[ji] = wsl[s].dma('pool', wb[s][:, :, :], Wv[:, :, jobs[ji][0]:jobs[ji][0] + 128])

            for ji in range(min(NW - 1, len(jobs))):
                issue_w(ji)
            bank = 0
            for ji, (col0, lo, hi, dst, scale) in enumerate(jobs):
                if ji + NW - 1 < len(jobs):
                    issue_w(ji + NW - 1)
                s = ji % NW
                sg = ji % NS
                evs = []
                for t in range(lo, hi, 512):
                    n = min(512, hi - t)
                    bk = bank % 6
                    bank += 1
                    kk.wait('pe', wtok[ji], pf_free[bk])
                    mm = None
                    for k in range(16):
                        mm = pe.matmul(pf[bk][:, 0:n], lhsT=wb[s][:, k, :], rhs=hT[:, k, t:t + n], start=(k == 0), stop=(k == 15))
                    mm = kk.mark('pe', mm)
                    eng = 'act' if (bank % 2 == 0) else 'dve'
                    kk.wait(eng, mm, ssl[sg].tok())
                    if eng == 'act':
                        ev = kk.mark('act', act.activation(out=stg[sg][:, t - lo:t - lo + n], in_=pf[bk][:, 0:n], func=AF.Copy, scale=float(scale)))
                    else:
                        ev = kk.mark('dve', dve.tensor_scalar(out=stg[sg][:, t - lo:t - lo + n], in0=pf[bk][:, 0:n], scalar1=float(scale), scalar2=None, op0=ALU.mult))
                    pf_free[bk] = ev
                    evs.append(ev)
                wfree[s] = mm
                kk.wait('sp', *evs)
                ssl[sg].dma('sp', dst, stg[sg][:, 0:hi - lo])
            barrier()

    def v_transposes(VT, items, Vdst):
        toks = []
        for g0 in range(0, len(items), 8):
            grp = items[g0:g0 + 8]
            bi = (g0 // 8) % 2
            kk.wait('pe', pb_free[bi])
            tp = None
            for ii, (src, n, di) in enumerate(grp):
                tp = pe.transpose(pb[bi][0:n, ii * 128:(ii + 1) * 128], src, ident_b[:])
            tp = kk.mark('pe', tp)
            eng = 'act' if bi == 0 else 'dve'
            kk.wait(eng, tp)
            d0 = grp[0][2]
            consecutive = all(grp[ii][2] == d0 + ii and grp[ii][1] == 128 for ii in range(len(grp)))
            if consecutive:
                src_ps = pb[bi][:, 0:len(grp) * 128].rearrange("p (a b) -> p a b", b=128)
                if eng == 'act':
                    ev = kk.mark('act', act.activation(out=Vdst[:, d0:d0 + len(grp), :], in_=src_ps, func=AF.Copy))
                else:
                    ev = kk.mark('dve', dve.tensor_copy(out=Vdst[:, d0:d0 + len(grp), :], in_=src_ps))
            else:
                ev = None
                for ii, (src, n, di) in enumerate(grp):
                    if eng == 'act':
                        ev = act.activation(out=Vdst[0:n, di, :], in_=pb[bi][0:n, ii * 128:(ii + 1) * 128], func=AF.Copy)
                    else:
                        ev = dve.tensor_copy(out=Vdst[0:n, di, :], in_=pb[bi][0:n, ii * 128:(ii + 1) * 128])
                ev = kk.mark(eng, ev)
            pb_free[bi] = ev
            toks.append(ev)
        return toks

    def layer0_front():
        with ExitStack() as ph:
            ph.enter_context(nc.named_scope('L0_normproj'))
            hT = sb("hT0", [128, 16, EXT], BF16, ph)
            with ExitStack() as p0:
                st = norm_state(p0)
                xb = [sb("xb0", [128, D], F32, p0), sb("xb1", [128, D], F32, p0)]
                xsl = [kk.slot(), kk.slot()]
                xfree = [None, None]
                tg = load_gain(0)
                ntile = EXT // 128
                xt_tok = {}

                def issue_x(i):
                    kk.wait('sp', xfree[i % 2])
                    xt_tok[i] = xsl[i % 2].dma('sp', xb[i % 2][:], x_ext[i * 128:(i + 1) * 128, :])

                issue_x(0)
                for i in range(ntile):
                    if i + 1 < ntile:
                        issue_x(i + 1)
                    rec = []
                    norm_tile(xb[i % 2][:], xt_tok[i], hT, i * 128, st, tg, xs_out=rec)
                    xfree[i % 2] = st['ss_free'][(st['i'] - 1) % 4]
                barrier()
            jobs = []
            for h in range(16):
                col0 = h * 128 if h < 8 else 3072 + (h - 8) * 128
                jobs.append((col0, QOFF, QOFF + NQ0, QT0[h * 128:(h + 1) * 128, :], QSCALE))
            for h in range(8):
                jobs.append((1024 + h * 128, AOFF, AOFF + NA, KaT[h * 128:(h + 1) * 128, :], 1.0))
                jobs.append((2048 + h * 128, AOFF, AOFF + NA, VaT[h * 128:(h + 1) * 128, :], 1.0))
            for h in range(8):
                jobs.append((4096 + h * 128, 0, EXT, KbT[h * 128:(h + 1) * 128, :], 1.0))
                jobs.append((5120 + h * 128, 0, EXT, VbT[h * 128:(h + 1) * 128, :], 1.0))
            proj_fm(hT, w_in0, jobs, ph, "p0")

        with ExitStack() as ph:
            ph.enter_context(nc.named_scope('L0_mixA'))
            NB = 2
            qT = [sb("a_q%d" % i, [128, NQ0], BF16, ph) for i in range(NB)]
            kT = [sb("a_k%d" % i, [128, NA], BF16, ph) for i in range(NB)]
            vT = [sb("a_vT%d" % i, [128, NA], BF16, ph) for i in range(NB)]
            vv = [sb("a_v%d" % i, [128, 22, 128], BF16, ph) for i in range(NB)]
            bA = [sb("a_b%d" % i, [128, 25 * 128], F32, ph) for i in range(NB)]
            oT = [sb("a_o%d" % i, [128, NQ0], BF16, ph) for i in range(NB)]
            tmp = [sb("a_t%d" % i, [128, 640], F32, ph) for i in range(3)]
            pT = [sb("a_p%d" % i, [128, 640], BF16, ph) for i in range(3)]
            rd = [sb("a_r%d" % i, [128, 128], F32, ph) for i in range(2)]
            lsl = [kk.slot() for _ in range(NB)]
            osl = [kk.slot() for _ in range(NB)]
            hfree = [None] * NB
            ltok = {}

            def issue_head(h):
                s = h % NB
                kk.wait('sp', hfree[s])
                lsl[s].dma('sp', qT[s][:], QT0[h * 128:(h + 1) * 128, :])
                lsl[s].dma('sp', kT[s][:], KaT[h * 128:(h + 1) * 128, :])
                lsl[s].dma('sp', vT[s][:], VaT[h * 128:(h + 1) * 128, :])
                ltok[h] = lsl[s].dma('sp', bA[s][:], biasA_d[h, :, :])

            issue_head(0)
            tmp_free = [None] * 3
            pT_free = [None] * 3
            rd_free = [None, None]
            te_tok = {}
            gidx = [0]
            for h in range(8):
                if h + 1 < 8:
                    issue_head(h + 1)
                s = h % NB
                kk.wait('pe', ltok[h])
                kk.wait('dve', ltok[h])
                vt = v_transposes(vT[s], [(vT[s][:, t * 128:(t + 1) * 128], 128, t) for t in range(22)], vv[s])
                kk.wait('pe', *vt)
                last_o = [None]

                def stage1(p, gi):
                    ty = A_P_TYPE.get(p, 0)
                    u = gi % 3
                    kk.wait('pe', pf_free[u], pf_free[3 + u])
                    mm = None
                    for c in range(5):
                        kc = 128 if c < 4 else 64
                        dstp = pf[u][0:kc, c * 128:(c + 1) * 128] if c < 4 else pf[3 + u][0:kc, 0:128]
                        mm = pe.matmul(dstp, lhsT=kT[s][:, 128 * (p + c):128 * (p + c) + kc], rhs=qT[s][:, 128 * p:128 * (p + 1)],
                                       start=True, stop=True)
                    mm = kk.mark('pe', mm)
                    kk.wait('dve', mm, tmp_free[u])
                    boff = ty * 640
                    dve.tensor_tensor(out=tmp[u][:, 0:512], in0=pf[u][:, :], in1=bA[s][:, boff:boff + 512], op=ALU.add)
                    ta = kk.mark('dve', dve.tensor_tensor(out=tmp[u][0:64, 512:640], in0=pf[3 + u][0:64, 0:128],
                                                          in1=bA[s][0:64, boff + 512:boff + 640], op=ALU.add))
                    pf_free[u] = ta
                    pf_free[3 + u] = ta
                    kk.wait('act', ta, pT_free[u])
                    act.activation(out=pT[u][:, 0:512], in_=tmp[u][:, 0:512], func=AF.Exp)
                    te = kk.mark('act', act.activation(out=pT[u][0:64, 512:640], in_=tmp[u][0:64, 512:640], func=AF.Exp))
                    tmp_free[u] = te
                    te_tok[gi] = te

                def stage2(p, gi):
                    u = gi % 3
                    w = gi % 2
                    bo = 6 + w
                    kk.wait('pe', te_tok.pop(gi), pf_free[bo])
                    for c in range(5):
                        kc = 128 if c < 4 else 64
                        pe.matmul(pf[bo][:, 0:128], lhsT=vv[s][0:kc, p + c, :], rhs=pT[u][0:kc, c * 128:(c + 1) * 128],
                                  start=(c == 0), stop=(c == 4))
                    mm2 = None
                    for c in range(5):
                        kc = 128 if c < 4 else 64
                        mm2 = pe.matmul(pf[bo][:, 128:256], lhsT=ones_b[0:kc, :], rhs=pT[u][0:kc, c * 128:(c + 1) * 128],
                                        start=(c == 0), stop=(c == 4))
                    mm2 = kk.mark('pe', mm2)
                    pT_free[u] = mm2
                    kk.wait('dve', mm2, rd_free[w], osl[s].tok())
                    tr = kk.mark('dve', dve.reciprocal(out=rd[w][:], in_=pf[bo][:, 128:256]))
                    kk.wait('dve', tr)
                    lo = kk.mark('dve', dve.tensor_tensor(out=oT[s][:, 128 * p:128 * (p + 1)], in0=pf[bo][:, 0:128], in1=rd[w][:], op=ALU.mult))
                    rd_free[w] = lo
                    pf_free[bo] = lo
                    last_o[0] = lo

                LAG = 2
                g0 = gidx[0]
                for i in range(18 + LAG):
                    if i < 18:
                        stage1(i, g0 + i)
                    if i - LAG >= 0:
                        stage2(i - LAG, g0 + i - LAG)
                gidx[0] += 18
                kk.wait('sp', last_o[0])
                osl[s].dma('sp', OT0[h * 128:(h + 1) * 128, :], oT[s][:])
                hfree[s] = last_o[0]
            barrier()

        pre_w0 = prefetch_wout(w_out0, "op0")
        with ExitStack() as ph:
            ph.enter_context(nc.named_scope('L0_mixB'))
            NB = 2
            qT = [sb("b_q%d" % i, [128, NQ0], BF16, ph) for i in range(NB)]
            kT = [sb("b_k%d" % i, [128, EXT], BF16, ph) for i in range(NB)]
            vT = [sb("b_vT%d" % i, [128, EXT], BF16, ph) for i in range(NB)]
            vv = sb("b_v", [128, 91, 128], BF16, ph)
            dmB = sb("b_dm", [128, NVB * 2 * 128], F32, ph)
            oacc = sb("b_oacc", [128, NQ0], F32, ph)
            dacc = sb("b_dacc", [128, NQ0], F32, ph)
            oT = [sb("b_o%d" % i, [128, NQ0], BF16, ph) for i in range(2)]
            tmp = [sb("b_t%d" % i, [128, 256], F32, ph) for i in range(3)]
            pT = [sb("b_p%d" % i, [128, 256], BF16, ph) for i in range(3)]
            lsl = [kk.slot() for _ in range(NB)]
            osl = [kk.slot() for _ in range(2)]
            dsl = kk.slot()
            t_dm = dsl.dma('sp', dmB[:], dmB_d[:, :])
            hfree = [None] * NB
            ltok = {}

            def issue_head(h):
                s = h % NB
                kk.wait('sp', hfree[s])
                lsl[s].dma('sp', qT[s][:], QT0[1024 + h * 128:1024 + (h + 1) * 128, :])
                lsl[s].dma('sp', kT[s][:], KbT[h * 128:(h + 1) * 128, :])
                ltok[h] = lsl[s].dma('sp', vT[s][:], VbT[h * 128:(h + 1) * 128, :])

            issue_head(0)
            tmp_free = [None] * 3
            pT_free = [None] * 3
            vv_free = None
            acc_free = None
            blk = 0
            for h in range(8):
                if h + 1 < 8:
                    issue_head(h + 1)
                s = h % NB
                slope = 2.0 ** (-(h + 1))
                kk.wait('pe', ltok[h], vv_free)
                kk.wait('act', vv_free)
                kk.wait('dve', ltok[h], t_dm, vv_free)
                items = []
                for t in range(7, 26):
                    items.append((vT[s][:, 64 + 128 * t:64 + 128 * (t + 1)], 128, t - 7))
                for rho in range(4):
                    for t in range(1, 7):
                        st0 = rho + 4 * (64 + 128 * t)
                        items.append((vT[s][:, sl(st0, 128, 4)], 128, 19 + rho * 6 + (t - 1)))
                for rho in range(16):
                    for t in range(3):
                        n = 128 if t < 2 else 16
                        st0 = rho + 16 * 128 * t
                        items.append((vT[s][:, sl(st0, n, 16)], n, 43 + rho * 3 + t))
                vt = v_transposes(vT[s], items, vv)
                kk.wait('pe', *vt)
                kk.wait('dve', acc_free)
                lastv = {'acc': None, 'pv': None}
                blocks = []
                for d in (1, 4, 16):
                    for rho in range(d):
                        for (b, qs, nq) in b_blocks(d):
                            blocks.append((d, rho, b, qs, nq))
                te_tok = {}

                def stage1(bd, gi):
                    d, rho, b, qs, nq = bd
                    var = b_variant(d, b)
                    u = gi % 3
                    bs = u
                    ks = qs - 64
                    kk.wait('pe', pf_free[bs])
                    mm = None
                    q0e = rho + d * qs - QOFF
                    for c in range(2):
                        kc = 128 if c == 0 else nq
                        k0e = rho + d * (ks + 128 * c)
                        mm = pe.matmul(pf[bs][0:kc, c * 128:c * 128 + nq],
                                       lhsT=kT[s][:, sl(k0e, kc, d)],
                                       rhs=qT[s][:, sl(q0e, nq, d)],
                                       start=True, stop=True)
                    mm = kk.mark('pe', mm)
                    kk.wait('dve', mm, tmp_free[u])
                    ta = None
                    for c in range(2):
                        kc = 128 if c == 0 else nq
                        doff = (var * 2 + c) * 128
                        ta = dve.scalar_tensor_tensor(out=tmp[u][0:kc, c * 128:c * 128 + nq], in0=dmB[0:kc, doff:doff + nq],
                                                      scalar=-slope, in1=pf[bs][0:kc, c * 128:c * 128 + nq],
                                                      op0=ALU.mult, op1=ALU.add)
                    ta = kk.mark('dve', ta)
                    pf_free[bs] = ta
                    kk.wait('act', ta, pT_free[u])
                    te = None
                    for c in range(2):
                        kc = 128 if c == 0 else nq
                        te = act.activation(out=pT[u][0:kc, c * 128:c * 128 + nq], in_=tmp[u][0:kc, c * 128:c * 128 + nq], func=AF.Exp)
                    te = kk.mark('act', te)
                    tmp_free[u] = te
                    te_tok[gi] = te

                def stage2(bd, gi):
                    d, rho, b, qs, nq = bd
                    u = gi % 3
                    bo = 3 + u
                    q0e = rho + d * qs - QOFF
                    kk.wait('pe', te_tok.pop(gi), pf_free[bo])
                    mm2 = None
                    for part in range(2):
                        for c in range(2):
                            kc = 128 if c == 0 else nq
                            if d == 1:
                                vi = b + c
                            elif d == 4:
                                vi = 19 + rho * 6 + (b + c)
                            else:
                                vi = 43 + rho * 3 + (b + c)
                            lhs = vv[0:kc, vi, :] if part == 0 else ones_b[0:kc, :]
                            mm2 = pe.matmul(pf[bo][:, part * 128:part * 128 + nq], lhsT=lhs, rhs=pT[u][0:kc, c * 128:c * 128 + nq],
                                            start=(c == 0), stop=(c == 1))
                    mm2 = kk.mark('pe', mm2)
                    lastv['pv'] = mm2
                    pT_free[u] = mm2
                    kk.wait('dve', mm2)
                    if d == 1:
                        dve.tensor_copy(out=oacc[:, q0e:q0e + nq], in_=pf[bo][:, 0:nq])
                        la = kk.mark('dve', dve.tensor_copy(out=dacc[:, q0e:q0e + nq], in_=pf[bo][:, 128:128 + nq]))
                    else:
                        oa = oacc[:, sl(q0e, nq, d)]
                        da = dacc[:, sl(q0e, nq, d)]
                        dve.tensor_tensor(out=oa, in0=pf[bo][:, 0:nq], in1=oa, op=ALU.add)
                        la = kk.mark('dve', dve.tensor_tensor(out=da, in0=pf[bo][:, 128:128 + nq], in1=da, op=ALU.add))
                    pf_free[bo] = la
                    lastv['acc'] = la

                LAG = 2
                nb = len(blocks)
                for i in range(nb + LAG):
                    if i < nb:
                        stage1(blocks[i], blk + i)
                    if i - LAG >= 0:
                        stage2(blocks[i - LAG], blk + i - LAG)
                blk += nb
                last_acc = lastv['acc']
                last_pv = lastv['pv']
                vv_free = last_pv
                so = h % 2
                kk.wait('dve', last_acc, osl[so].tok())
                tr = kk.mark('dve', dve.reciprocal(out=dacc[:], in_=dacc[:]))
                kk.wait('dve', tr)
                to = kk.mark('dve', dve.tensor_tensor(out=oT[so][:], in0=oacc[:], in1=dacc[:], op=ALU.mult))
                acc_free = to
                kk.wait('sp', to)
                osl[so].dma('sp', OT0[1024 + h * 128:1024 + (h + 1) * 128, :], oT[so][:])
                hfree[s] = last_pv
            barrier()


        return pre_w0

    def prefetch_wout(Wout, name):
        wst = ExitStack()
        wo = sb(name + "_wo", [128, 16, D], BF16, wst)
        wsl = kk.slot(name + "_wsl", persistent=True)
        Wv = Wout.rearrange("(k p) c -> p k c", p=128)
        for k in range(16):
            wsl.dma('pool', wo[:, k, :], Wv[:, k, :])
        return wo, wsl.tok(), wst

    def out_proj(Wout, OTd, ntok, x_src_fn, Xdst, name, pre=None):
        with ExitStack() as ph:
            ph.enter_context(nc.named_scope(name))
            if pre is None:
                pre = prefetch_wout(Wout, name)
            wo, t_w, wst = pre
            ph.callback(wst.close)
            ot = [sb(name + "_ot%d" % i, [128, 16, 128], BF16, ph) for i in range(2)]
            xt = [sb(name + "_x%d" % i, [128, D], F32, ph) for i in range(2)]
            xo = [sb(name + "_xo%d" % i, [128, D], F32, ph) for i in range(2)]
            lsl = [kk.slot() for _ in range(2)]
            osl = [kk.slot() for _ in range(2)]
            lfree = [None, None]
            ltok = {}
            OTv = OTd.rearrange("(k p) t -> p k t", p=128)

            def issue(i):
                s = i % 2
                kk.wait('sp', lfree[s])
                lsl[s].dma('sp', ot[s][:, :, :], OTv[:, :, i * 128:(i + 1) * 128])
                ltok[i] = lsl[s].dma('sp', xt[s][:], x_src_fn(i))

            nt = ntok // 128
            issue(0)
            kk.wait('pe', t_w)
            bank = 0
            for i in range(nt):
                if i + 1 < nt:
                    issue(i + 1)
                s = i % 2
                kk.wait('pe', ltok[i])
                kk.wait('dve', ltok[i], osl[s].tok())
                ev = None
                mm = None
                for j in range(4):
                    bk = bank % 6
                    bank += 1
                    kk.wait('pe', pf_free[bk])
                    for k in range(16):
                        mm = pe.matmul(pf[bk][:, :], lhsT=ot[s][:, k, :], rhs=wo[:, k, 512 * j:512 * (j + 1)], start=(k == 0), stop=(k == 15))
                    mm = kk.mark('pe', mm)
                    kk.wait('dve', mm)
                    ev = kk.mark('dve', dve.tensor_tensor(out=xo[s][:, 512 * j:512 * (j + 1)], in0=pf[bk][:, :],
                                                          in1=xt[s][:, 512 * j:512 * (j + 1)], op=ALU.add))
                    pf_free[bk] = ev
                lfree[s] = ev
                kk.wait('sp', ev)
                osl[s].dma('sp', Xdst[i * 128:(i + 1) * 128, :], xo[s][:])
            barrier()

    def mlp(layer, Xsrc, ntok, Xdst, final, name):
        W1 = w1s[layer]
        W2 = w2s[layer]
        with ExitStack() as ph:
            ph.enter_context(nc.named_scope(name))
            G = MLP_G
            NT = G // 128
            x1 = sb(name + "_x1", [128, NT, D], F32, ph)
            h2T = sb(name + "_h2T", [128, 16, G], BF16, ph)
            aT = sb(name + "_aT", [128, 32, G], BF16, ph)
            NW1 = 2
            w1b = [sb(name + "_w1%d" % i, [128, 16, 512], BF16, ph) for i in range(NW1)]
            w1sl = [kk.slot() for _ in range(NW1)]
            w1free = [None] * NW1
            NW2 = 4
            w2b = [sb(name + "_w2%d" % i, [128, 4, 512], BF16, ph) for i in range(NW2)]
            w2sl = [kk.slot() for _ in range(NW2)]
            w2free = [None] * NW2
            rb = [sb(name + "_r%d" % i, [128, 512], F32, ph) for i in range(2)]
            rfree = [None, None]
            st = norm_state(ph)
            xsl = [kk.slot() for _ in range(NT)]
            osl = [kk.slot() for _ in range(NT)]
            W1v = W1.rearrange("(k p) c -> p k c", p=128)
            W2v = W2.rearrange("(k p) c -> p k c", p=128)
            tg = load_gain(2 + layer)
            tgf = load_gain(4, grep2) if final else None
            groups = []
            t = 0
            while t < ntok:
                groups.append((t, min(G, ntok - t)))
                t += G
            jobs = []
            for g in range(len(groups)):
                for half in range(2):
                    for wbk in range(8):
                        jobs.append(('w1', g, half, wbk, 0))
                    for j in range(4):
                        for q in range(8):
                            jobs.append(('w2', g, half, j, q))
            wtok = {}
            cnt = {'w1': 0, 'w2': 0}
            nxt = [0]

            done = {'w1': 0, 'w2': 0}
            ring = {'w1': NW1, 'w2': NW2}

            def issue_next():
                if nxt[0] >= len(jobs):
                    return False
                job = jobs[nxt[0]]
                kind, g, half, a1, a2 = job
                if cnt[kind] - done[kind] >= ring[kind]:
                    return False
                if kind == 'w1':
                    s = cnt['w1'] % NW1
                    kk.wait('pool', w1free[s])
                    c0 = half * 4096 + a1 * 512
                    wtok[job] = (w1sl[s].dma('pool', w1b[s][:, :, :], W1v[:, :, c0:c0 + 512]), s)
                    cnt['w1'] += 1
                else:
                    s = cnt['w2'] % NW2
                    kk.wait('pool', w2free[s])
                    k0 = half * 32 + a2 * 4
                    wtok[job] = (w2sl[s].dma('pool', w2b[s][:, :, :], W2v[:, k0:k0 + 4, a1 * 512:(a1 + 1) * 512]), s)
                    cnt['w2'] += 1
                nxt[0] += 1
                return True

            def ensure(upto):
                while nxt[0] <= upto and issue_next():
                    pass

            x1_free = [None] * NT
            h2_free = None
            aT_free = None
            bank1 = 0
            jpos = 0
            for g, (t0, n) in enumerate(groups):
                nti = n // 128
                t_x = []
                for i in range(nti):
                    kk.wait('sp', x1_free[i])
                    t_x.append(xsl[i].dma('sp', x1[:, i, :], Xsrc[t0 + i * 128:t0 + (i + 1) * 128, :]))
                ensure(jpos + 1)
                evs = []
                for i in range(nti):
                    evs += norm_tile(x1[:, i, :], t_x[i], h2T, i * 128, st, tg, extra_wait=h2_free)
                kk.wait('pe', *evs)
                last_add = None
                add_tok = [None] * NT
                for half in (range(2) if MLP_DBG >= 2 else []):
                    last_sq = None
                    last_mm = None
                    for wbk in range(8):
                        ensure(jpos + 6)
                        tk, s = wtok[('w1', g, half, wbk, 0)]
                        jpos += 1
                        kk.wait('pe', tk)
                        for fc in range(4):
                            fi = 4 * wbk + fc
                            for ts in range(0, n, 512):
                                nn = min(512, n - ts)
                                bk = P1BANK + bank1 % 2
                                rbi = bank1 % 2
                                bank1 += 1
                                kk.wait('pe', pf_free[bk], aT_free if (fi == 0) else None)
                                mm = None
                                for k in range(16):
                                    mm = pe.matmul(pf[bk][:, 0:nn], lhsT=w1b[s][:, k, fc * 128:(fc + 1) * 128], rhs=h2T[:, k, ts:ts + nn],
                                                   start=(k == 0), stop=(k == 15))
                                mm = kk.mark('pe', mm)
                                last_mm = mm
                                kk.wait('act', mm, rfree[rbi])
                                tr = kk.mark('act', act.activation(out=rb[rbi][:, 0:nn], in_=pf[bk][:, 0:nn], func=AF.Relu))
                                pf_free[bk] = tr
                                kk.wait('dve', tr, aT_free if (fi == 0) else None)
                                last_sq = kk.mark('dve', dve.tensor_tensor(out=aT[:, fi, ts:ts + nn], in0=rb[rbi][:, 0:nn], in1=rb[rbi][:, 0:nn], op=ALU.mult))
                                rfree[rbi] = last_sq
                        w1free[s] = last_mm
                        done['w1'] += 1
                    if half == 1:
                        h2_free = last_mm
                    kk.wait('pe', last_sq)
                    last_mm2 = None
                    for j in (range(4) if MLP_DBG >= 3 else []):
                        for i in range(nti):
                            kk.wait('pe', pf_free[i])
                        for q in range(8):
                            ensure(jpos + 6)
                            tk, s = wtok[('w2', g, half, j, q)]
                            jpos += 1
                            kk.wait('pe', tk)
                            mm = None
                            for fc in range(4):
                                fi = 4 * q + fc
                                for i in range(nti):
                                    mm = pe.matmul(pf[i][:, :], lhsT=aT[:, fi, i * 128:(i + 1) * 128], rhs=w2b[s][:, fc, :],
                                                   start=(fi == 0), stop=(fi == 31))
                            mm = kk.mark('pe', mm)
                            w2free[s] = mm
                            done['w2'] += 1
                            last_mm2 = mm
                        kk.wait('dve', last_mm2)
                        for i in range(nti):
                            xs = x1[:, i, 512 * j:512 * (j + 1)]
                            last_add = kk.mark('dve', dve.tensor_tensor(out=xs, in0=pf[i][:, :], in1=xs, op=ALU.add))
                            pf_free[i] = last_add
                            add_tok[i] = last_add
                    aT_free = last_mm2 if last_mm2 is not None else aT_free
                if not final:
                    for i in range(nti):
                        kk.wait('sp', add_tok[i], *evs)
                        osl[i].dma('sp', Xdst[t0 + i * 128:t0 + (i + 1) * 128, :], x1[:, i, :])
                        x1_free[i] = osl[i].tok()
                else:
                    for i in range(nti):
                        kk.wait('act', add_tok[i])
                        ii = i % 4
                        ss = small[:, 24 + ii:25 + ii]
                        sd = small[:, 28 + ii:29 + ii]
                        rs = small[:, 32 + ii:33 + ii]
                        t1 = kk.mark('act', act.activation(out=st['junk'][:], in_=x1[:, i, :], func=AF.Square, accum_out=ss))
                        kk.wait('act', t1)
                        t2 = kk.mark('act', act.activation(out=sd, in_=ss, func=AF.Sqrt, bias=st['eps'][:, 0:1], scale=1.0 / D))
                        kk.wait('dve', t2)
                        t3 = kk.mark('dve', dve.reciprocal(out=rs, in_=sd))
                        kk.wait('dve', t3, tgf)
                        t4 = kk.mark('dve', dve.scalar_tensor_tensor(out=x1[:, i, :], in0=x1[:, i, :], scalar=rs, in1=grep2[:], op0=ALU.mult, op1=ALU.mult))
                        kk.wait('sp', t4)
                        osl[i].dma('sp', Xdst[t0 + i * 128:t0 + (i + 1) * 128, :], x1[:, i, :])
                        x1_free[i] = osl[i].tok()
                        kk.wait('act', t4)
            barrier()

    def layer1_front():
        with ExitStack() as ph:
            ph.enter_context(nc.named_scope('L1_normproj'))
            hT = sb("hT1", [128, 16, NQ0], BF16, ph)
            with ExitStack() as p0:
                st = norm_state(p0)
                xb = [sb("xc0", [128, D], F32, p0), sb("xc1", [128, D], F32, p0)]
                xsl = [kk.slot(), kk.slot()]
                xfree = [None, None]
                tg = load_gain(1)
                ntile = NQ0 // 128
                xt_tok = {}

                def issue_x1(i):
                    kk.wait('sp', xfree[i % 2])
                    xt_tok[i] = xsl[i % 2].dma('sp', xb[i % 2][:], X1[i * 128:(i + 1) * 128, :])

                issue_x1(0)
                for i in range(ntile):
                    if i + 1 < ntile:
                        issue_x1(i + 1)
                    norm_tile(xb[i % 2][:], xt_tok[i], hT, i * 128, st, tg)
                    xfree[i % 2] = st['ss_free'][(st['i'] - 1) % 4]
                barrier()
            jobs = []
            for h in range(16):
                jobs.append((h * 128, 128, 128 + OWN, QT1[h * 128:(h + 1) * 128, :], QSCALE))
            for g in range(4):
                jobs.append((2048 + g * 128, 0, NQ0, KT1[g * 128:(g + 1) * 128, :], 1.0))
                jobs.append((2560 + g * 128, 0, NQ0, VT1[g * 128:(g + 1) * 128, :], 1.0))
            proj_fm(hT, w_qkv1, jobs, ph, "p1")

        pre_w1 = prefetch_wout(w_out1, "op1")
        with ExitStack() as ph:
            ph.enter_context(nc.named_scope('L1_attn'))
            NB = 2
            qT = [sb("c_q%d" % i, [128, 4, OWN], BF16, ph) for i in range(NB)]
            kT = [sb("c_k%d" % i, [128, NQ0], BF16, ph) for i in range(NB)]
            vT = [sb("c_vT%d" % i, [128, NQ0], BF16, ph) for i in range(NB)]
            vv = [sb("c_v%d" % i, [128, 18, 128], BF16, ph) for i in range(NB)]
            oT = [sb("c_o%d" % i, [128, 4, OWN], BF16, ph) for i in range(NB)]
            dmC = sb("c_dm", [128, 9 * 128], F32, ph)
            tmp = [sb("c_t%d" % i, [128, 3, 512], F32, ph) for i in range(2)]
            pT = [sb("c_p%d" % i, [128, 3, 512], BF16, ph) for i in range(2)]
            dn = [sb("c_dn%d" % i, [128, 512], F32, ph) for i in range(2)]
            skr = sb("c_sk", [1, 16], F32, ph)
            esk = sb("c_esk", [128, 16], F32, ph)
            lsl = [kk.slot() for _ in range(NB)]
            osl = [kk.slot() for _ in range(NB)]
            dsl = kk.slot()
            dsl.dma('sp', dmC[:], dmC_d[:, :])
            t_dm = dsl.dma('sp', skr[:], sink_d[:, :])
            kk.wait('pe', t_dm)
            tsk = kk.mark('pe', pe.matmul(pf[5][:, 0:16], lhsT=ones_f[0:1, :], rhs=skr[0:1, :], start=True, stop=True))
            kk.wait('act', tsk)
            t_esk = kk.mark('act', act.activation(out=esk[:], in_=pf[5][:, 0:16], func=AF.Exp))
            pf_free[5] = t_esk
            hfree = [None] * NB
            ltok = {}
            QT1v = QT1.rearrange("(h p) t -> p h t", p=128)
            OT1v = OT1.rearrange("(h p) t -> p h t", p=128)

            def issue_grp(g):
                s = g % NB
                kk.wait('sp', hfree[s])
                lsl[s].dma('sp', qT[s][:, :, :], QT1v[:, 4 * g:4 * g + 4, :])
                lsl[s].dma('sp', kT[s][:], KT1[g * 128:(g + 1) * 128, :])
                ltok[g] = lsl[s].dma('sp', vT[s][:], VT1[g * 128:(g + 1) * 128, :])

            issue_grp(0)
            tmp_free = [None, None]
            pT_free = [None, None]
            dn_free = [None, None]
            te_tok = {}
            gidx = [0]
            for g in range(4):
                if g + 1 < 4:
                    issue_grp(g + 1)
                s = g % NB
                kk.wait('pe', ltok[g])
                kk.wait('dve', ltok[g], t_dm, t_esk, osl[s].tok())
                vt = v_transposes(vT[s], [(vT[s][:, t * 128:(t + 1) * 128], 128, t) for t in range(18)], vv[s])
                kk.wait('pe', *vt)
                last_o = [None]

                def stage1(b, gi):
                    var = 0 if b == 0 else (2 if b == 15 else 1)
                    u = gi % 2
                    sb0 = 3 * u
                    mm = None
                    for c in range(3):
                        kk.wait('pe', pf_free[sb0 + c])
                        mm = pe.matmul(pf[sb0 + c][:, :], lhsT=kT[s][:, 128 * (b + c):128 * (b + c + 1)], rhs=qT[s][:, :, 128 * b:128 * (b + 1)],
                                       start=True, stop=True)
                    mm = kk.mark('pe', mm)
                    kk.wait('dve', mm, tmp_free[u])
                    ta = None
                    for c in range(3):
                        for hh in range(4):
                            slope = 2.0 ** (-(4 * g + hh + 1) / 2.0)
                            doff = (var * 3 + c) * 128
                            ta = dve.scalar_tensor_tensor(out=tmp[u][:, c, hh * 128:(hh + 1) * 128], in0=dmC[:, doff:doff + 128],
                                                          scalar=-slope, in1=pf[sb0 + c][:, hh * 128:(hh + 1) * 128], op0=ALU.mult, op1=ALU.add)
                        ta = kk.mark('dve', ta)
                        pf_free[sb0 + c] = ta
                    kk.wait('act', ta, pT_free[u])
                    te = kk.mark('act', act.activation(out=pT[u][:, :, :], in_=tmp[u][:, :, :], func=AF.Exp))
                    tmp_free[u] = te
                    te_tok[gi] = te

                def stage2(b, gi):
                    u = gi % 2
                    kk.wait('pe', te_tok.pop(gi), pf_free[6], pf_free[7])
                    for c in range(3):
                        pe.matmul(pf[6][:, :], lhsT=vv[s][:, b + c, :], rhs=pT[u][:, c, :], start=(c == 0), stop=(c == 2))
                    mm2 = None
                    for c in range(3):
                        mm2 = pe.matmul(pf[7][:, :], lhsT=ones_b[:, :], rhs=pT[u][:, c, :], start=(c == 0), stop=(c == 2))
                    mm2 = kk.mark('pe', mm2)
                    pT_free[u] = mm2
                    kk.wait('dve', mm2, dn_free[u])
                    for hh in range(4):
                        hi = 4 * g + hh
                        dve.tensor_scalar(out=dn[u][:, hh * 128:(hh + 1) * 128], in0=pf[7][:, hh * 128:(hh + 1) * 128],
                                          scalar1=esk[:, hi:hi + 1], scalar2=None, op0=ALU.add)
                    tr = kk.mark('dve', dve.reciprocal(out=dn[u][:], in_=dn[u][:]))
                    pf_free[7] = tr
                    kk.wait('dve', tr)
                    lo = kk.mark('dve', dve.tensor_tensor(out=oT[s][:, :, 128 * b:128 * (b + 1)],
                                                          in0=pf[6][:, :].rearrange("p (h t) -> p h t", h=4),
                                                          in1=dn[u][:, :].rearrange("p (h t) -> p h t", h=4), op=ALU.mult))
                    dn_free[u] = lo
                    pf_free[6] = lo
                    last_o[0] = lo

                LAG = 1
                g0 = gidx[0]
                for i in range(16 + LAG):
                    if i < 16:
                        stage1(i, g0 + i)
                    if i - LAG >= 0:
                        stage2(i - LAG, g0 + i - LAG)
                gidx[0] += 16
                kk.wait('sp', last_o[0])
                osl[s].dma('sp', OT1v[:, 4 * g:4 * g + 4, :], oT[s][:, :, :])
                hfree[s] = last_o[0]
            barrier()


        return pre_w1

    if MODE == 'mlp_only':
        mlp(0, x_ext[QOFF + 128:QOFF + 128 + OWN, :], OWN, y_out, False, "m0")
    elif MODE == 'l0_only':
        pre0 = layer0_front()
        out_proj(w_out0, OT0, NQ0, lambda i: x_ext[QOFF + i * 128:QOFF + (i + 1) * 128, :], X1a, "op0", pre0)
        mlp(0, X1a[128:128 + OWN, :], OWN, y_out, False, "m0")
    else:
        pre0 = layer0_front()
        out_proj(w_out0, OT0, NQ0, lambda i: x_ext[QOFF + i * 128:QOFF + (i + 1) * 128, :], X1a, "op0", pre0)
        mlp(0, X1a, NQ0, X1, False, "m0")
        pre1 = layer1_front()
        out_proj(w_out1, OT1, OWN, lambda i: X1[128 + i * 128:128 + (i + 1) * 128, :], X2a, "op1", pre1)
        mlp(1, X2a, OWN, y_out, True, "m1")
    es.close()
    return nc


_CACHE = {}


def kernel(x, attn_norm, mlp_norm, w_mlp_in, w_mlp_out, even_w_in, even_rpb, even_w_out,
           odd_w_qkv, odd_sink, odd_w_out, final_norm):
    x = np.asarray(x, np.float32)
    f = lambda a: np.ascontiguousarray(np.asarray(a, np.float32))
    if 'nc' not in _CACHE:
        _CACHE['nc'] = build_program()
    nc = _CACHE['nc']
    gains = np.stack([f(attn_norm)[0], f(attn_norm)[1], f(mlp_norm)[0], f(mlp_norm)[1], f(final_norm)], 0)
    rpb = f(even_rpb)[0]
    shared = {
        "gains": np.ascontiguousarray(gains),
        "w_in0": f(even_w_in)[0], "w_out0": f(even_w_out)[0],
        "w_qkv1": f(odd_w_qkv)[0], "w_out1": f(odd_w_out)[0],
        "w1_0": f(w_mlp_in)[0], "w1_1": f(w_mlp_in)[1],
        "w2_0": f(w_mlp_out)[0], "w2_1": f(w_mlp_out)[1],
        "sink": f(odd_sink).reshape(1, 16),
        "ident": np.eye(128, dtype=np.float32),
    }
    in_maps = []
    for c in range(NCORE):
        biasA, dmB, dmC = host_tables(c, rpb)
        m = dict(shared)
        m["x_ext"] = host_x_ext(c, x)
        m["biasA"] = np.ascontiguousarray(biasA.reshape(8, 128, 25 * 128))
        m["dmB"] = np.ascontiguousarray(dmB.reshape(128, NVB * 2 * 128))
        m["dmC"] = np.ascontiguousarray(dmC.reshape(128, 9 * 128))
        in_maps.append(m)
    res = run_bass_kernel_spmd(nc, in_maps, core_ids=list(range(NCORE)))
    if DEBUG:
        _CACHE['res'] = res
    out = np.zeros((2, SEQ, D), np.float32)
    for c in range(NCORE):
        out[c // 4, (c % 4) * OWN:(c % 4 + 1) * OWN] = res.results[c]["y"]
    return out
```

```python
import numpy as np
from contextlib import ExitStack
import concourse.bass as bass
import concourse.mybir as mybir
from concourse.bass_utils import run_bass_kernel_spmd

F32 = mybir.dt.float32
BF16 = mybir.dt.bfloat16
AF = mybir.ActivationFunctionType
ALU = mybir.AluOpType

D = 2048
SEQ = 8192
DFF = 8192
NCORE = 8
OWN = 2048
NQ0 = 2304
EXT = 4352
QOFF = 1024
AOFF = 768
NA = 2816
BIG = 1e30
EPS = 1e-6
QSCALE = 128.0 ** -0.5
DEBUG = False
MODE = None
MLP_DBG = 3
MLP_G = 768
P1BANK = 6

B_VARIANT = {1: {0: 0, 1: 1, 16: 3, 17: 4}, 4: {0: 5, 1: 6, 2: 6, 3: 7, 4: 8}, 16: {0: 9, 1: 10}}
NVB = 11


def b_variant(d, b):
    if d == 1:
        return B_VARIANT[1].get(b, 2)
    return B_VARIANT[d][b]


def b_blocks(d):
    q0 = QOFF // d
    qn = NQ0 // d
    out = []
    b = 0
    while 128 * b < qn:
        out.append((b, q0 + 128 * b, min(128, qn - 128 * b)))
        b += 1
    return out


A_TYPE_P = {0: 8, 1: 1, 2: 2, 3: 15, 4: 16}
A_P_TYPE = {1: 1, 2: 2, 15: 3, 16: 4}


def host_tables(c, rpb):
    cq = c % 4
    T0 = cq * OWN
    q = np.arange(128)
    k = np.arange(128)
    biasA = np.full((8, 128, 5, 5, 128), -BIG, np.float32)
    for ty, p in A_TYPE_P.items():
        r = 32 * cq - 2 + 2 * p
        qr = q // 64
        qc = q % 64
        gq = r + qr
        for ch in range(5):
            kr = -4 + 2 * ch + k // 64
            kcol = k % 64
            cand = r + kr
            g = np.where(cand < 0, cand + 8, np.where(cand >= 128, cand - 8, cand))
            rel = kr[:, None] - qr[None, :]
            okr = (rel >= -4) & (rel <= 3)
            qreal = (gq >= 0) & (gq < 128)
            gk = np.where(qreal[None, :], g[:, None], cand[:, None])
            dr = gk - gq[None, :]
            okr = okr & (dr >= -7) & (dr <= 7)
            c0 = np.clip(qc - 8, 0, 48)
            okc = (kcol[:, None] >= c0[None, :]) & (kcol[:, None] < c0[None, :] + 16)
            dc = kcol[:, None] - qc[None, :]
            ok = okr & okc
            if ch == 4:
                ok = ok & (k[:, None] < 64)
            dri = np.clip(dr + 7, 0, 14)
            dci = np.clip(dc + 15, 0, 30)
            for h in range(8):
                vals = rpb[h][dri, dci]
                biasA[h, :, ty, ch, :] = np.where(ok, vals, -BIG)
    dmB = np.full((128, NVB, 2, 128), BIG, np.float32)
    for d in (1, 4, 16):
        for (b, qs, nq) in b_blocks(d):
            v = b_variant(d, b)
            i = np.arange(128)
            for ch in range(2):
                j = 128 * ch + k
                dist = np.abs(i[None, :] + 64 - j[:, None])
                gk = T0 - 1152 + d * (qs - 64 + j)
                gqq = T0 - 1152 + d * (qs + i)
                kval = (gk >= 0) & (gk < SEQ)
                qval = (gqq >= 0) & (gqq < SEQ)
                ok = (dist <= 64) & (kval[:, None] | ~qval[None, :])
                dmB[:, v, ch, :] = np.where(ok, (d * dist).astype(np.float32), BIG)
    dmC = np.full((128, 3, 3, 128), BIG, np.float32)
    for v, b in ((0, 0), (1, 8), (2, 15)):
        i = np.arange(128)
        for ch in range(3):
            j = 128 * ch + k
            dist = np.abs(128 + i[None, :] - j[:, None])
            gk = T0 - 128 + 128 * b + j
            kval = (gk >= 0) & (gk < SEQ)
            ok = (dist <= 128) & kval[:, None]
            dmC[:, v, ch, :] = np.where(ok, dist.astype(np.float32), BIG)
    return biasA, dmB, dmC


def host_x_ext(c, x):
    bi, cq = c // 4, c % 4
    T0 = cq * OWN
    xe = np.zeros((EXT, D), np.float32)
    g0 = T0 - 1152
    lo, hi = max(g0, 0), min(g0 + EXT, SEQ)
    xe[lo - g0:hi - g0] = x[bi, lo:hi]
    if cq == 0:
        xe[-256 - g0:0 - g0] = x[bi, 256:512]
    if cq == 3:
        xe[SEQ - g0:SEQ + 192 - g0] = x[bi, SEQ - 512:SEQ - 320]
    return xe


class Slot:
    def __init__(self, kk, name):
        self.kk = kk
        self.sem = kk.es.enter_context(kk.nc.semaphore(name))
        self.val = 0

    def dma(self, q, out, in_):
        assert q == getattr(self, 'q', q), (q, self.q)
        self.kk.engs[q].dma_start(out=out, in_=in_).then_inc(self.sem, 16)
        self.val += 16
        return ('d', self, self.val)

    def tok(self):
        return ('d', self, self.val) if self.val else None


class KK:
    def __init__(self, nc, es):
        self.nc = nc
        self.es = es
        self.engs = {'pe': nc.tensor, 'act': nc.scalar, 'dve': nc.vector, 'pool': nc.gpsimd, 'sp': nc.sync}
        self.sem = {}
        self.cnt = {}
        for e in ('pe', 'act', 'dve', 'pool'):
            self.sem[e] = es.enter_context(nc.semaphore('m_' + e))
            self.cnt[e] = 0
        self.waited = {}
        self.slots = []
        self.nslot = 0
        self.pool = {'sp': [], 'pool': []}
        self.phase_slots = []

    def slot(self, name=None, persistent=False, q='sp'):
        if not persistent and self.pool[q]:
            s = self.pool[q].pop()
            self.phase_slots.append(s)
            return s
        self.nslot += 1
        s = Slot(self, name or ('sl%d' % self.nslot))
        s.q = q
        self.slots.append(s)
        if not persistent:
            self.phase_slots.append(s)
        return s

    def recycle(self):
        for s in self.phase_slots:
            self.pool[s.q].append(s)
        self.phase_slots = []

    def mark(self, e, instr):
        instr.then_inc(self.sem[e], 1)
        self.cnt[e] += 1
        return ('c', e, self.cnt[e])

    def wait(self, e, *toks):
        for tok in toks:
            if tok is None:
                continue
            if isinstance(tok, list):
                self.wait(e, *tok)
                continue
            kind, key, val = tok
            if kind == 'c':
                semh, kid = self.sem[key], key
            else:
                semh, kid = key.sem, id(key)
            if self.waited.get((e, kid), 0) >= val:
                continue
            self.engs[e].wait_ge(semh, val)
            self.waited[(e, kid)] = val


def sl(start, n, d=1):
    return slice(start, start + d * (n - 1) + 1, d)


def build_program():
    nc = bass.Bass("TRN2", target_bir_lowering=False)
    es = ExitStack()
    kk = KK(nc, es)
    pe, act, dve, pool, sp = nc.tensor, nc.scalar, nc.vector, nc.gpsimd, nc.sync

    def din(name, shape, dt=F32):
        return nc.dram_tensor(name, list(shape), dt, kind="ExternalInput").ap()

    def dscr(name, shape, dt):
        return nc.dram_tensor(name, list(shape), dt, kind=("ExternalOutput" if DEBUG else "Internal")).ap()

    x_ext = din("x_ext", [EXT, D])
    gains = din("gains", [5, D])
    w_in0 = din("w_in0", [D, 6144])
    w_out0 = din("w_out0", [D, D])
    w_qkv1 = din("w_qkv1", [D, 3072])
    w_out1 = din("w_out1", [D, D])
    w1s = [din("w1_0", [D, DFF]), din("w1_1", [D, DFF])]
    w2s = [din("w2_0", [DFF, D]), din("w2_1", [DFF, D])]
    biasA_d = din("biasA", [8, 128, 25 * 128])
    dmB_d = din("dmB", [128, NVB * 2 * 128])
    dmC_d = din("dmC", [128, 9 * 128])
    sink_d = din("sink", [1, 16])
    ident_d = din("ident", [128, 128])
    y_out = nc.dram_tensor("y", [OWN, D], F32, kind="ExternalOutput").ap()

    QT0 = dscr("QT0", [2048, NQ0], BF16)
    KaT = dscr("KaT", [1024, NA], BF16)
    VaT = dscr("VaT", [1024, NA], BF16)
    KbT = dscr("KbT", [1024, EXT], BF16)
    VbT = dscr("VbT", [1024, EXT], BF16)
    OT0 = dscr("OT0", [2048, NQ0], BF16)
    X1a = dscr("X1a", [NQ0, D], F32)
    X1 = dscr("X1", [NQ0, D], F32)
    QT1 = dscr("QT1", [2048, OWN], BF16)
    KT1 = dscr("KT1", [512, NQ0], BF16)
    VT1 = dscr("VT1", [512, NQ0], BF16)
    OT1 = dscr("OT1", [2048, OWN], BF16)
    X2a = dscr("X2a", [OWN, D], F32)

    uniq = [0]

    def sb(name, shape, dt, stack=None):
        uniq[0] += 1
        return (stack or es).enter_context(nc.sbuf_tensor("%s_%d" % (name, uniq[0]), list(shape), dt))

    ident_f = sb("ident_f", [128, 128], F32)
    ident_b = sb("ident_b", [128, 128], BF16)
    ones_b = sb("ones_b", [128, 128], BF16)
    ones_f = sb("ones_f", [1, 128], F32)
    grep = sb("grep", [128, D], F32)
    grep2 = sb("grep2", [128, D], F32)
    dummy = sb("dmy_t", [128, 8], F32)
    small = sb("small", [128, 64], F32)
    pf = [es.enter_context(nc.psum_tensor("pf%d" % i, [128, 512], F32)) for i in range(8)]

    class _PB:
        def __init__(self, t):
            self.t = t

        def __getitem__(self, idx):
            return self.t[:, :].bitcast(BF16)[idx]

    pb = [_PB(pf[6]), _PB(pf[7])]

    class _Alias:
        def __getitem__(self, i):
            return pf_free[6 + i]

        def __setitem__(self, i, v):
            pf_free[6 + i] = v

    pf_free = [None] * 8
    pb_free = _Alias()

    c_slot = kk.slot("const", persistent=True)
    c_slot.dma('sp', ident_f[:], ident_d[:, :])
    t_c = c_slot.tok()
    kk.wait('dve', t_c)
    dve.tensor_copy(out=ident_b[:], in_=ident_f[:])
    dve.memset(ones_b[:], 1.0)
    dve.memset(dummy[:], 0.0)
    t_init = kk.mark('dve', dve.memset(ones_f[:], 1.0))

    def barrier():
        toks = []
        for s in kk.slots:
            kk.wait('sp', s.tok())
            kk.wait('pool', s.tok())
        kk.wait('pe', t_init, pf_free[0])
        toks.append(kk.mark('pe', pe.matmul(pf[0][0:8, 0:8], lhsT=ident_b[0:8, 0:8], rhs=ident_b[0:8, 0:8],
                                            start=True, stop=True)))
        toks.append(kk.mark('act', act.activation(out=dummy[:, 0:1], in_=dummy[:, 4:5], func=AF.Copy)))
        toks.append(kk.mark('dve', dve.tensor_copy(out=dummy[:, 1:2], in_=dummy[:, 5:6])))
        toks.append(kk.mark('pool', pool.tensor_copy(out=dummy[:, 2:3], in_=dummy[:, 6:7])))
        for e in ('pe', 'act', 'dve', 'pool', 'sp'):
            kk.wait(e, *toks)
        for i in range(8):
            pf_free[i] = None
        kk.recycle()

    g_slot = kk.slot("gain", persistent=True)
    g_slot2 = kk.slot("gain2", persistent=True)

    def load_gain(gi, dst=None):
        dst = grep if dst is None else dst
        return (g_slot if dst is grep else g_slot2).dma('sp', dst[:], gains[gi, :].partition_broadcast(128))

    def norm_tile(xt, t_x, hT, tcol, st, t_gain, extra_wait=None, xs_out=None):
        i = st['i']
        st['i'] += 1
        hb = st['hb'][i % 2]
        jk = st['junk']
        ss = small[:, 8 + (i % 4): 9 + (i % 4)]
        sd = small[:, 12 + (i % 4): 13 + (i % 4)]
        rs = small[:, 16 + (i % 4): 17 + (i % 4)]
        kk.wait('act', t_x, st.get('junk_free'), st['ss_free'][i % 4])
        t1 = kk.mark('act', act.activation(out=jk[:], in_=xt, func=AF.Square, accum_out=ss))
        kk.wait('act', t1)
        t2 = kk.mark('act', act.activation(out=sd, in_=ss, func=AF.Sqrt, bias=st['eps'][:, 0:1], scale=1.0 / D))
        kk.wait('dve', t2)
        t3 = kk.mark('dve', dve.reciprocal(out=rs, in_=sd))
        kk.wait('dve', t3, t_gain, st['hb_free'][i % 2], extra_wait)
        t4 = kk.mark('dve', dve.scalar_tensor_tensor(out=hb[:], in0=xt, scalar=rs, in1=grep[:], op0=ALU.mult, op1=ALU.mult))
        st['ss_free'][i % 4] = t4
        if xs_out is not None:
            xs_out.append(t3)
        kk.wait('pe', t4, pb_free[0], pb_free[1])
        tp = None
        for k in range(16):
            tp = pe.transpose(pb[k // 8][:, (k % 8) * 128:(k % 8 + 1) * 128], hb[:, k * 128:(k + 1) * 128], ident_b[:])
        tp = kk.mark('pe', tp)
        st['hb_free'][i % 2] = tp
        kk.wait('act', tp)
        kk.wait('dve', tp)
        e0 = kk.mark('act', act.activation(out=hT[:, 0:8, tcol:tcol + 128],
                                           in_=pb[0][:, :].rearrange("p (k t) -> p k t", k=8), func=AF.Copy))
        e1 = kk.mark('dve', dve.tensor_copy(out=hT[:, 8:16, tcol:tcol + 128],
                                            in_=pb[1][:, :].rearrange("p (k t) -> p k t", k=8)))
        pb_free[0] = e0
        pb_free[1] = e1
        return [e0, e1]

    def norm_state(stack):
        st = {'i': 0}
        st['hb'] = [sb("hb0", [128, D], BF16, stack), sb("hb1", [128, D], BF16, stack)]
        st['junk'] = sb("junk", [128, D], BF16, stack)
        st['eps'] = sb("epsb", [128, 1], F32, stack)
        st['ss_free'] = [None] * 4
        st['hb_free'] = [None, None]
        st['eps_tok'] = kk.mark('dve', dve.memset(st['eps'][:], EPS))
        kk.wait('act', st['eps_tok'])
        return st

    def proj_fm(hT, W, jobs, stack_parent, wname):
        with ExitStack() as st:
            NW = 3
            wb = [sb("%s_w%d" % (wname, i), [128, 16, 128], BF16, st) for i in range(NW)]
            wsl = [kk.slot(q='pool') for _ in range(NW)]
            wfree = [None] * NW
            maxn = max(j[2] - j[1] for j in jobs)
            NS = 2
            stg = [sb("%s_s%d" % (wname, i), [128, maxn], BF16, st) for i in range(NS)]
            ssl = [kk.slot() for _ in range(NS)]
            Wv = W.rearrange("(k p) c -> p k c", p=128)
            wtok = {}

            def issue_w(ji):
                s = ji % NW
                kk.wait('pool', wfree[s])
                wtok[ji] = wsl[s].dma('pool', wb[s][:, :, :], Wv[:, :, jobs[ji][0]:jobs[ji][0] + 128])

            for ji in range(min(NW - 1, len(jobs))):
                issue_w(ji)
            bank = 0
            for ji, (col0, lo, hi, dst, scale) in enumerate(jobs):
                if ji + NW - 1 < len(jobs):
                    issue_w(ji + NW - 1)
                s = ji % NW
                sg = ji % NS
                evs = []
                for t in range(lo, hi, 512):
                    n = min(512, hi - t)
                    bk = bank % 6
                    bank += 1
                    kk.wait('pe', wtok[ji], pf_free[bk])
                    mm = None
                    for k in range(16):
                        mm = pe.matmul(pf[bk][:, 0:n], lhsT=wb[s][:, k, :], rhs=hT[:, k, t:t + n], start=(k == 0), stop=(k == 15))
                    mm = kk.mark('pe', mm)
                    eng = 'act' if (bank % 2 == 0) else 'dve'
                    kk.wait(eng, mm, ssl[sg].tok())
                    if eng == 'act':
                        ev = kk.mark('act', act.activation(out=stg[sg][:, t - lo:t - lo + n], in_=pf[bk][:, 0:n], func=AF.Copy, scale=float(scale)))
                    else:
                        ev = kk.mark('dve', dve.tensor_scalar(out=stg[sg][:, t - lo:t - lo + n], in0=pf[bk][:, 0:n], scalar1=float(scale), scalar2=None, op0=ALU.mult))
                    pf_free[bk] = ev
                    evs.append(ev)
                wfree[s] = mm
                kk.wait('sp', *evs)
                ssl[sg].dma('sp', dst, stg[sg][:, 0:hi - lo])
            barrier()

    def v_transposes(VT, items, Vdst):
        toks = []
        for g0 in range(0, len(items), 8):
            grp = items[g0:g0 + 8]
            bi = (g0 // 8) % 2
            kk.wait('pe', pb_free[bi])
            tp = None
            for ii, (src, n, di) in enumerate(grp):
                tp = pe.transpose(pb[bi][0:n, ii * 128:(ii + 1) * 128], src, ident_b[:])
            tp = kk.mark('pe', tp)
            eng = 'act' if bi == 0 else 'dve'
            kk.wait(eng, tp)
            d0 = grp[0][2]
            consecutive = all(grp[ii][2] == d0 + ii and grp[ii][1] == 128 for ii in range(len(grp)))
            if consecutive:
                src_ps = pb[bi][:, 0:len(grp) * 128].rearrange("p (a b) -> p a b", b=128)
                if eng == 'act':
                    ev = kk.mark('act', act.activation(out=Vdst[:, d0:d0 + len(grp), :], in_=src_ps, func=AF.Copy))
                else:
                    ev = kk.mark('dve', dve.tensor_copy(out=Vdst[:, d0:d0 + len(grp), :], in_=src_ps))
            else:
                ev = None
                for ii, (src, n, di) in enumerate(grp):
                    if eng == 'act':
                        ev = act.activation(out=Vdst[0:n, di, :], in_=pb[bi][0:n, ii * 128:(ii + 1) * 128], func=AF.Copy)
                    else:
                        ev = dve.tensor_copy(out=Vdst[0:n, di, :], in_=pb[bi][0:n, ii * 128:(ii + 1) * 128])
                ev = kk.mark(eng, ev)
            pb_free[bi] = ev
            toks.append(ev)
        return toks

    def layer0_front():
        with ExitStack() as ph:
            ph.enter_context(nc.named_scope('L0_normproj'))
            hT = sb("hT0", [128, 16, EXT], BF16, ph)
            with ExitStack() as p0:
                st = norm_state(p0)
                xb = [sb("xb0", [128, D], F32, p0), sb("xb1", [128, D], F32, p0)]
                xsl = [kk.slot(), kk.slot()]
                xfree = [None, None]
                tg = load_gain(0)
                ntile = EXT // 128
                xt_tok = {}

                def issue_x(i):
                    kk.wait('sp', xfree[i % 2])
                    xt_tok[i] = xsl[i % 2].dma('sp', xb[i % 2][:], x_ext[i * 128:(i + 1) * 128, :])

                issue_x(0)
                for i in range(ntile):
                    if i + 1 < ntile:
                        issue_x(i + 1)
                    rec = []
                    norm_tile(xb[i % 2][:], xt_tok[i], hT, i * 128, st, tg, xs_out=rec)
                    xfree[i % 2] = st['ss_free'][(st['i'] - 1) % 4]
                barrier()
            jobs = []
            for h in range(16):
                col0 = h * 128 if h < 8 else 3072 + (h - 8) * 128
                jobs.append((col0, QOFF, QOFF + NQ0, QT0[h * 128:(h + 1) * 128, :], QSCALE))
            for h in range(8):
                jobs.append((1024 + h * 128, AOFF, AOFF + NA, KaT[h * 128:(h + 1) * 128, :], 1.0))
                jobs.append((2048 + h * 128, AOFF, AOFF + NA, VaT[h * 128:(h + 1) * 128, :], 1.0))
            for h in range(8):
                jobs.append((4096 + h * 128, 0, EXT, KbT[h * 128:(h + 1) * 128, :], 1.0))
                jobs.append((5120 + h * 128, 0, EXT, VbT[h * 128:(h + 1) * 128, :], 1.0))
            proj_fm(hT, w_in0, jobs, ph, "p0")

        with ExitStack() as ph:
            ph.enter_context(nc.named_scope('L0_mixA'))
            NB = 2
            qT = [sb("a_q%d" % i, [128, NQ0], BF16, ph) for i in range(NB)]
            kT = [sb("a_k%d" % i, [128, NA], BF16, ph) for i in range(NB)]
            vT = [sb("a_vT%d" % i, [128, NA], BF16, ph) for i in range(NB)]
            vv = [sb("a_v%d" % i, [128, 22, 128], BF16, ph) for i in range(NB)]
            bA = [sb("a_b%d" % i, [128, 25 * 128], F32, ph) for i in range(NB)]
            oT = [sb("a_o%d" % i, [128, NQ0], BF16, ph) for i in range(NB)]
            tmp = [sb("a_t%d" % i, [128, 640], F32, ph) for i in range(3)]
            pT = [sb("a_p%d" % i, [128, 640], BF16, ph) for i in range(3)]
            rd = [sb("a_r%d" % i, [128, 128], F32, ph) for i in range(2)]
            lsl = [kk.slot() for _ in range(NB)]
            osl = [kk.slot() for _ in range(NB)]
            hfree = [None] * NB
            ltok = {}

            def issue_head(h):
                s = h % NB
                kk.wait('sp', hfree[s])
                lsl[s].dma('sp', qT[s][:], QT0[h * 128:(h + 1) * 128, :])
                lsl[s].dma('sp', kT[s][:], KaT[h * 128:(h + 1) * 128, :])
                lsl[s].dma('sp', vT[s][:], VaT[h * 128:(h + 1) * 128, :])
                ltok[h] = lsl[s].dma('sp', bA[s][:], biasA_d[h, :, :])

            issue_head(0)
            tmp_free = [None] * 3
            pT_free = [None] * 3
            rd_free = [None, None]
            te_tok = {}
            gidx = [0]
            for h in range(8):
                if h + 1 < 8:
                    issue_head(h + 1)
                s = h % NB
                kk.wait('pe', ltok[h])
                kk.wait('dve', ltok[h])
                vt = v_transposes(vT[s], [(vT[s][:, t * 128:(t + 1) * 128], 128, t) for t in range(22)], vv[s])
                kk.wait('pe', *vt)
                last_o = [None]

                def stage1(p, gi):
                    ty = A_P_TYPE.get(p, 0)
                    u = gi % 3
                    kk.wait('pe', pf_free[u], pf_free[3 + u])
                    mm = None
                    for c in range(5):
                        kc = 128 if c < 4 else 64
                        dstp = pf[u][0:kc, c * 128:(c + 1) * 128] if c < 4 else pf[3 + u][0:kc, 0:128]
                        mm = pe.matmul(dstp, lhsT=kT[s][:, 128 * (p + c):128 * (p + c) + kc], rhs=qT[s][:, 128 * p:128 * (p + 1)],
                                       start=True, stop=True)
                    mm = kk.mark('pe', mm)
                    kk.wait('dve', mm, tmp_free[u])
                    boff = ty * 640
                    dve.tensor_tensor(out=tmp[u][:, 0:512], in0=pf[u][:, :], in1=bA[s][:, boff:boff + 512], op=ALU.add)
                    ta = kk.mark('dve', dve.tensor_tensor(out=tmp[u][0:64, 512:640], in0=pf[3 + u][0:64, 0:128],
                                                          in1=bA[s][0:64, boff + 512:boff + 640], op=ALU.add))
                    pf_free[u] = ta
                    pf_free[3 + u] = ta
                    kk.wait('act', ta, pT_free[u])
                    act.activation(out=pT[u][:, 0:512], in_=tmp[u][:, 0:512], func=AF.Exp)
                    te = kk.mark('act', act.activation(out=pT[u][0:64, 512:640], in_=tmp[u][0:64, 512:640], func=AF.Exp))
                    tmp_free[u] = te
                    te_tok[gi] = te

                def stage2(p, gi):
                    u = gi % 3
                    w = gi % 2
                    bo = 6 + w
                    kk.wait('pe', te_tok.pop(gi), pf_free[bo])
                    for c in range(5):
                        kc = 128 if c < 4 else 64
                        pe.matmul(pf[bo][:, 0:128], lhsT=vv[s][0:kc, p + c, :], rhs=pT[u][0:kc, c * 128:(c + 1) * 128],
                                  start=(c == 0), stop=(c == 4))
                    mm2 = None
                    for c in range(5):
                        kc = 128 if c < 4 else 64
                        mm2 = pe.matmul(pf[bo][:, 128:256], lhsT=ones_b[0:kc, :], rhs=pT[u][0:kc, c * 128:(c + 1) * 128],
                                        start=(c == 0), stop=(c == 4))
                    mm2 = kk.mark('pe', mm2)
                    pT_free[u] = mm2
                    kk.wait('dve', mm2, rd_free[w], osl[s].tok())
                    tr = kk.mark('dve', dve.reciprocal(out=rd[w][:], in_=pf[bo][:, 128:256]))
                    kk.wait('dve', tr)
                    lo = kk.mark('dve', dve.tensor_tensor(out=oT[s][:, 128 * p:128 * (p + 1)], in0=pf[bo][:, 0:128], in1=rd[w][:], op=ALU.mult))
                    rd_free[w] = lo
                    pf_free[bo] = lo
                    last_o[0] = lo

                LAG = 2
                g0 = gidx[0]
                for i in range(18 + LAG):
                    if i < 18:
                        stage1(i, g0 + i)
                    if i - LAG >= 0:
                        stage2(i - LAG, g0 + i - LAG)
                gidx[0] += 18
                kk.wait('sp', last_o[0])
                osl[s].dma('sp', OT0[h * 128:(h + 1) * 128, :], oT[s][:])
                hfree[s] = last_o[0]
            barrier()

        pre_w0 = prefetch_wout(w_out0, "op0")
        with ExitStack() as ph:
            ph.enter_context(nc.named_scope('L0_mixB'))
            NB = 2
            qT = [sb("b_q%d" % i, [128, NQ0], BF16, ph) for i in range(NB)]
            kT = [sb("b_k%d" % i, [128, EXT], BF16, ph) for i in range(NB)]
            vT = [sb("b_vT%d" % i, [128, EXT], BF16, ph) for i in range(NB)]
            vv = sb("b_v", [128, 91, 128], BF16, ph)
            dmB = sb("b_dm", [128, NVB * 2 * 128], F32, ph)
            oacc = sb("b_oacc", [128, NQ0], F32, ph)
            dacc = sb("b_dacc", [128, NQ0], F32, ph)
            oT = [sb("b_o%d" % i, [128, NQ0], BF16, ph) for i in range(2)]
            tmp = [sb("b_t%d" % i, [128, 256], F32, ph) for i in range(3)]
            pT = [sb("b_p%d" % i, [128, 256], BF16, ph) for i in range(3)]
            lsl = [kk.slot() for _ in range(NB)]
            osl = [kk.slot() for _ in range(2)]
            dsl = kk.slot()
            t_dm = dsl.dma('sp', dmB[:], dmB_d[:, :])
            hfree = [None] * NB
            ltok = {}

            def issue_head(h):
                s = h % NB
                kk.wait('sp', hfree[s])
                lsl[s].dma('sp', qT[s][:], QT0[1024 + h * 128:1024 + (h + 1) * 128, :])
                lsl[s].dma('sp', kT[s][:], KbT[h * 128:(h + 1) * 128, :])
                ltok[h] = lsl[s].dma('sp', vT[s][:], VbT[h * 128:(h + 1) * 128, :])

            issue_head(0)
            tmp_free = [None] * 3
            pT_free = [None] * 3
            vv_free = None
            acc_free = None
            blk = 0
            for h in range(8):
                if h + 1 < 8:
                    issue_head(h + 1)
                s = h % NB
                slope = 2.0 ** (-(h + 1))
                kk.wait('pe', ltok[h], vv_free)
                kk.wait('act', vv_free)
                kk.wait('dve', ltok[h], t_dm, vv_free)
                items = []
                for t in range(7, 26):
                    items.append((vT[s][:, 64 + 128 * t:64 + 128 * (t + 1)], 128, t - 7))
                for rho in range(4):
                    for t in range(1, 7):
                        st0 = rho + 4 * (64 + 128 * t)
                        items.append((vT[s][:, sl(st0, 128, 4)], 128, 19 + rho * 6 + (t - 1)))
                for rho in range(16):
                    for t in range(3):
                        n = 128 if t < 2 else 16
                        st0 = rho + 16 * 128 * t
                        items.append((vT[s][:, sl(st0, n, 16)], n, 43 + rho * 3 + t))
                vt = v_transposes(vT[s], items, vv)
                kk.wait('pe', *vt)
                kk.wait('dve', acc_free)
                lastv = {'acc': None, 'pv': None}
                blocks = []
                for d in (1, 4, 16):
                    for rho in range(d):
                        for (b, qs, nq) in b_blocks(d):
                            blocks.append((d, rho, b, qs, nq))
                te_tok = {}

                def stage1(bd, gi):
                    d, rho, b, qs, nq = bd
                    var = b_variant(d, b)
                    u = gi % 3
                    bs = u
                    ks = qs - 64
                    kk.wait('pe', pf_free[bs])
                    mm = None
                    q0e = rho + d * qs - QOFF
                    for c in range(2):
                        kc = 128 if c == 0 else nq
                        k0e = rho + d * (ks + 128 * c)
                        mm = pe.matmul(pf[bs][0:kc, c * 128:c * 128 + nq],
                                       lhsT=kT[s][:, sl(k0e, kc, d)],
                                       rhs=qT[s][:, sl(q0e, nq, d)],
                                       start=True, stop=True)
                    mm = kk.mark('pe', mm)
                    kk.wait('dve', mm, tmp_free[u])
                    ta = None
                    for c in range(2):
                        kc = 128 if c == 0 else nq
                        doff = (var * 2 + c) * 128
                        ta = dve.scalar_tensor_tensor(out=tmp[u][0:kc, c * 128:c * 128 + nq], in0=dmB[0:kc, doff:doff + nq],
                                                      scalar=-slope, in1=pf[bs][0:kc, c * 128:c * 128 + nq],
                                                      op0=ALU.mult, op1=ALU.add)
                    ta = kk.mark('dve', ta)
                    pf_free[bs] = ta
                    kk.wait('act', ta, pT_free[u])
                    te = None
                    for c in range(2):
                        kc = 128 if c == 0 else nq
                        te = act.activation(out=pT[u][0:kc, c * 128:c * 128 + nq], in_=tmp[u][0:kc, c * 128:c * 128 + nq], func=AF.Exp)
                    te = kk.mark('act', te)
                    tmp_free[u] = te
                    te_tok[gi] = te

                def stage2(bd, gi):
                    d, rho, b, qs, nq = bd
                    u = gi % 3
                    bo = 3 + u
                    q0e = rho + d * qs - QOFF
                    kk.wait('pe', te_tok.pop(gi), pf_free[bo])
                    mm2 = None
                    for part in range(2):
                        for c in range(2):
                            kc = 128 if c == 0 else nq
                            if d == 1:
                                vi = b + c
                            elif d == 4:
                                vi = 19 + rho * 6 + (b + c)
                            else:
                                vi = 43 + rho * 3 + (b + c)
                            lhs = vv[0:kc, vi, :] if part == 0 else ones_b[0:kc, :]
                            mm2 = pe.matmul(pf[bo][:, part * 128:part * 128 + nq], lhsT=lhs, rhs=pT[u][0:kc, c * 128:c * 128 + nq],
                                            start=(c == 0), stop=(c == 1))
                    mm2 = kk.mark('pe', mm2)
                    lastv['pv'] = mm2
                    pT_free[u] = mm2
                    kk.wait('dve', mm2)
                    if d == 1:
                        dve.tensor_copy(out=oacc[:, q0e:q0e + nq], in_=pf[bo][:, 0:nq])
                        la = kk.mark('dve', dve.tensor_copy(out=dacc[:, q0e:q0e + nq], in_=pf[bo][:, 128:128 + nq]))
                    else:
                        oa = oacc[:, sl(q0e, nq, d)]
                        da = dacc[:, sl(q0e, nq, d)]
                        dve.tensor_tensor(out=oa, in0=pf[bo][:, 0:nq], in1=oa, op=ALU.add)
                        la = kk.mark('dve', dve.tensor_tensor(out=da, in0=pf[bo][:, 128:128 + nq], in1=da, op=ALU.add))
                    pf_free[bo] = la
                    lastv['acc'] = la

                LAG = 2
                nb = len(blocks)
                for i in range(nb + LAG):
                    if i < nb:
                        stage1(blocks[i], blk + i)
                    if i - LAG >= 0:
                        stage2(blocks[i - LAG], blk + i - LAG)
                blk += nb
                last_acc = lastv['acc']
                last_pv = lastv['pv']
                vv_free = last_pv
                so = h % 2
                kk.wait('dve', last_acc, osl[so].tok())
                tr = kk.mark('dve', dve.reciprocal(out=dacc[:], in_=dacc[:]))
                kk.wait('dve', tr)
                to = kk.mark('dve', dve.tensor_tensor(out=oT[so][:], in0=oacc[:], in1=dacc[:], op=ALU.mult))
                acc_free = to
                kk.wait('sp', to)
                osl[so].dma('sp', OT0[1024 + h * 128:1024 + (h + 1) * 128, :], oT[so][:])
                hfree[s] = last_pv
            barrier()


        return pre_w0

    def prefetch_wout(Wout, name):
        wst = ExitStack()
        wo = sb(name + "_wo", [128, 16, D], BF16, wst)
        wsl = kk.slot(name + "_wsl", persistent=True, q='pool')
        Wv = Wout.rearrange("(k p) c -> p k c", p=128)
        for k in range(16):
            wsl.dma('pool', wo[:, k, :], Wv[:, k, :])
        return wo, wsl.tok(), wst

    def out_proj(Wout, OTd, ntok, x_src_fn, Xdst, name, pre=None):
        with ExitStack() as ph:
            ph.enter_context(nc.named_scope(name))
            if pre is None:
                pre = prefetch_wout(Wout, name)
            wo, t_w, wst = pre
            ph.callback(wst.close)
            ot = [sb(name + "_ot%d" % i, [128, 16, 128], BF16, ph) for i in range(2)]
            xt = [sb(name + "_x%d" % i, [128, D], F32, ph) for i in range(2)]
            xo = [sb(name + "_xo%d" % i, [128, D], F32, ph) for i in range(2)]
            lsl = [kk.slot() for _ in range(2)]
            osl = [kk.slot() for _ in range(2)]
            lfree = [None, None]
            ltok = {}
            OTv = OTd.rearrange("(k p) t -> p k t", p=128)

            def issue(i):
                s = i % 2
                kk.wait('sp', lfree[s])
                lsl[s].dma('sp', ot[s][:, :, :], OTv[:, :, i * 128:(i + 1) * 128])
                ltok[i] = lsl[s].dma('sp', xt[s][:], x_src_fn(i))

            nt = ntok // 128
            issue(0)
            kk.wait('pe', t_w)
            bank = 0
            for i in range(nt):
                if i + 1 < nt:
                    issue(i + 1)
                s = i % 2
                kk.wait('pe', ltok[i])
                kk.wait('dve', ltok[i], osl[s].tok())
                ev = None
                mm = None
                for j in range(4):
                    bk = bank % 6
                    bank += 1
                    kk.wait('pe', pf_free[bk])
                    for k in range(16):
                        mm = pe.matmul(pf[bk][:, :], lhsT=ot[s][:, k, :], rhs=wo[:, k, 512 * j:512 * (j + 1)], start=(k == 0), stop=(k == 15))
                    mm = kk.mark('pe', mm)
                    kk.wait('dve', mm)
                    ev = kk.mark('dve', dve.tensor_tensor(out=xo[s][:, 512 * j:512 * (j + 1)], in0=pf[bk][:, :],
                                                          in1=xt[s][:, 512 * j:512 * (j + 1)], op=ALU.add))
                    pf_free[bk] = ev
                lfree[s] = ev
                kk.wait('sp', ev)
                osl[s].dma('sp', Xdst[i * 128:(i + 1) * 128, :], xo[s][:])
            barrier()

    def mlp(layer, Xsrc, ntok, Xdst, final, name):
        W1 = w1s[layer]
        W2 = w2s[layer]
        with ExitStack() as ph:
            ph.enter_context(nc.named_scope(name))
            G = MLP_G
            NT = G // 128
            x1 = sb(name + "_x1", [128, NT, D], F32, ph)
            h2T = sb(name + "_h2T", [128, 16, G], BF16, ph)
            aT = sb(name + "_aT", [128, 32, G], BF16, ph)
            NW1 = 2
            w1b = [sb(name + "_w1%d" % i, [128, 16, 512], BF16, ph) for i in range(NW1)]
            w1sl = [kk.slot(q='pool') for _ in range(NW1)]
            w1free = [None] * NW1
            NW2 = 4
            w2b = [sb(name + "_w2%d" % i, [128, 4, 512], BF16, ph) for i in range(NW2)]
            w2sl = [kk.slot(q='pool') for _ in range(NW2)]
            w2free = [None] * NW2
            rb = [sb(name + "_r%d" % i, [128, 512], F32, ph) for i in range(2)]
            rfree = [None, None]
            st = norm_state(ph)
            xsl = [kk.slot() for _ in range(NT)]
            osl = [kk.slot() for _ in range(NT)]
            W1v = W1.rearrange("(k p) c -> p k c", p=128)
            W2v = W2.rearrange("(k p) c -> p k c", p=128)
            tg = load_gain(2 + layer)
            tgf = load_gain(4, grep2) if final else None
            groups = []
            t = 0
            while t < ntok:
                groups.append((t, min(G, ntok - t)))
                t += G
            jobs = []
            for g in range(len(groups)):
                for half in range(2):
                    for wbk in range(8):
                        jobs.append(('w1', g, half, wbk, 0))
                    for j in range(4):
                        for q in range(8):
                            jobs.append(('w2', g, half, j, q))
            wtok = {}
            cnt = {'w1': 0, 'w2': 0}
            nxt = [0]

            done = {'w1': 0, 'w2': 0}
            ring = {'w1': NW1, 'w2': NW2}

            def issue_next():
                if nxt[0] >= len(jobs):
                    return False
                job = jobs[nxt[0]]
                kind, g, half, a1, a2 = job
                if cnt[kind] - done[kind] >= ring[kind]:
                    return False
                if kind == 'w1':
                    s = cnt['w1'] % NW1
                    kk.wait('pool', w1free[s])
                    c0 = half * 4096 + a1 * 512
                    wtok[job] = (w1sl[s].dma('pool', w1b[s][:, :, :], W1v[:, :, c0:c0 + 512]), s)
                    cnt['w1'] += 1
                else:
                    s = cnt['w2'] % NW2
                    kk.wait('pool', w2free[s])
                    k0 = half * 32 + a2 * 4
                    wtok[job] = (w2sl[s].dma('pool', w2b[s][:, :, :], W2v[:, k0:k0 + 4, a1 * 512:(a1 + 1) * 512]), s)
                    cnt['w2'] += 1
                nxt[0] += 1
                return True

            def ensure(upto):
                while nxt[0] <= upto and issue_next():
                    pass

            x1_free = [None] * NT
            h2_free = None
            aT_free = None
            bank1 = 0
            jpos = 0
            for g, (t0, n) in enumerate(groups):
                nti = n // 128
                t_x = []
                for i in range(nti):
                    kk.wait('sp', x1_free[i])
                    t_x.append(xsl[i].dma('sp', x1[:, i, :], Xsrc[t0 + i * 128:t0 + (i + 1) * 128, :]))
                ensure(jpos + 1)
                evs = []
                for i in range(nti):
                    evs += norm_tile(x1[:, i, :], t_x[i], h2T, i * 128, st, tg, extra_wait=h2_free)
                kk.wait('pe', *evs)
                last_add = None
                add_tok = [None] * NT
                for half in (range(2) if MLP_DBG >= 2 else []):
                    last_sq = None
                    last_mm = None
                    for wbk in range(8):
                        ensure(jpos + 6)
                        tk, s = wtok[('w1', g, half, wbk, 0)]
                        jpos += 1
                        kk.wait('pe', tk)
                        for fc in range(4):
                            fi = 4 * wbk + fc
                            for ts in range(0, n, 512):
                                nn = min(512, n - ts)
                                bk = P1BANK + bank1 % 2
                                rbi = bank1 % 2
                                bank1 += 1
                                kk.wait('pe', pf_free[bk], aT_free if (fi == 0) else None)
                                mm = None
                                for k in range(16):
                                    mm = pe.matmul(pf[bk][:, 0:nn], lhsT=w1b[s][:, k, fc * 128:(fc + 1) * 128], rhs=h2T[:, k, ts:ts + nn],
                                                   start=(k == 0), stop=(k == 15))
                                mm = kk.mark('pe', mm)
                                last_mm = mm
                                kk.wait('act', mm, rfree[rbi])
                                tr = kk.mark('act', act.activation(out=rb[rbi][:, 0:nn], in_=pf[bk][:, 0:nn], func=AF.Relu))
                                pf_free[bk] = tr
                                kk.wait('dve', tr, aT_free if (fi == 0) else None)
                                last_sq = kk.mark('dve', dve.tensor_tensor(out=aT[:, fi, ts:ts + nn], in0=rb[rbi][:, 0:nn], in1=rb[rbi][:, 0:nn], op=ALU.mult))
                                rfree[rbi] = last_sq
                        w1free[s] = last_mm
                        done['w1'] += 1
                    if half == 1:
                        h2_free = last_mm
                    kk.wait('pe', last_sq)
                    last_mm2 = None
                    for j in (range(4) if MLP_DBG >= 3 else []):
                        for i in range(nti):
                            kk.wait('pe', pf_free[i])
                        for q in range(8):
                            ensure(jpos + 6)
                            tk, s = wtok[('w2', g, half, j, q)]
                            jpos += 1
                            kk.wait('pe', tk)
                            mm = None
                            for fc in range(4):
                                fi = 4 * q + fc
                                for i in range(nti):
                                    mm = pe.matmul(pf[i][:, :], lhsT=aT[:, fi, i * 128:(i + 1) * 128], rhs=w2b[s][:, fc, :],
                                                   start=(fi == 0), stop=(fi == 31))
                            mm = kk.mark('pe', mm)
                            w2free[s] = mm
                            done['w2'] += 1
                            last_mm2 = mm
                        kk.wait('dve', last_mm2)
                        for i in range(nti):
                            xs = x1[:, i, 512 * j:512 * (j + 1)]
                            last_add = kk.mark('dve', dve.tensor_tensor(out=xs, in0=pf[i][:, :], in1=xs, op=ALU.add))
                            pf_free[i] = last_add
                            add_tok[i] = last_add
                    aT_free = last_mm2 if last_mm2 is not None else aT_free
                if not final:
                    for i in range(nti):
                        kk.wait('sp', add_tok[i], *evs)
                        osl[i].dma('sp', Xdst[t0 + i * 128:t0 + (i + 1) * 128, :], x1[:, i, :])
                        x1_free[i] = osl[i].tok()
                else:
                    for i in range(nti):
                        kk.wait('act', add_tok[i])
                        ii = i % 4
                        ss = small[:, 24 + ii:25 + ii]
                        sd = small[:, 28 + ii:29 + ii]
                        rs = small[:, 32 + ii:33 + ii]
                        t1 = kk.mark('act', act.activation(out=st['junk'][:], in_=x1[:, i, :], func=AF.Square, accum_out=ss))
                        kk.wait('act', t1)
                        t2 = kk.mark('act', act.activation(out=sd, in_=ss, func=AF.Sqrt, bias=st['eps'][:, 0:1], scale=1.0 / D))
                        kk.wait('dve', t2)
                        t3 = kk.mark('dve', dve.reciprocal(out=rs, in_=sd))
                        kk.wait('dve', t3, tgf)
                        t4 = kk.mark('dve', dve.scalar_tensor_tensor(out=x1[:, i, :], in0=x1[:, i, :], scalar=rs, in1=grep2[:], op0=ALU.mult, op1=ALU.mult))
                        kk.wait('sp', t4)
                        osl[i].dma('sp', Xdst[t0 + i * 128:t0 + (i + 1) * 128, :], x1[:, i, :])
                        x1_free[i] = osl[i].tok()
                        kk.wait('act', t4)
            barrier()

    def layer1_front():
        with ExitStack() as ph:
            ph.enter_context(nc.named_scope('L1_normproj'))
            hT = sb("hT1", [128, 16, NQ0], BF16, ph)
            with ExitStack() as p0:
                st = norm_state(p0)
                xb = [sb("xc0", [128, D], F32, p0), sb("xc1", [128, D], F32, p0)]
                xsl = [kk.slot(), kk.slot()]
                xfree = [None, None]
                tg = load_gain(1)
                ntile = NQ0 // 128
                xt_tok = {}

                def issue_x1(i):
                    kk.wait('sp', xfree[i % 2])
                    xt_tok[i] = xsl[i % 2].dma('sp', xb[i % 2][:], X1[i * 128:(i + 1) * 128, :])

                issue_x1(0)
                for i in range(ntile):
                    if i + 1 < ntile:
                        issue_x1(i + 1)
                    norm_tile(xb[i % 2][:], xt_tok[i], hT, i * 128, st, tg)
                    xfree[i % 2] = st['ss_free'][(st['i'] - 1) % 4]
                barrier()
            jobs = []
            for h in range(16):
                jobs.append((h * 128, 128, 128 + OWN, QT1[h * 128:(h + 1) * 128, :], QSCALE))
            for g in range(4):
                jobs.append((2048 + g * 128, 0, NQ0, KT1[g * 128:(g + 1) * 128, :], 1.0))
                jobs.append((2560 + g * 128, 0, NQ0, VT1[g * 128:(g + 1) * 128, :], 1.0))
            proj_fm(hT, w_qkv1, jobs, ph, "p1")

        pre_w1 = prefetch_wout(w_out1, "op1")
        with ExitStack() as ph:
            ph.enter_context(nc.named_scope('L1_attn'))
            NB = 2
            qT = [sb("c_q%d" % i, [128, 4, OWN], BF16, ph) for i in range(NB)]
            kT = [sb("c_k%d" % i, [128, NQ0], BF16, ph) for i in range(NB)]
            vT = [sb("c_vT%d" % i, [128, NQ0], BF16, ph) for i in range(NB)]
            vv = [sb("c_v%d" % i, [128, 18, 128], BF16, ph) for i in range(NB)]
            oT = [sb("c_o%d" % i, [128, 4, OWN], BF16, ph) for i in range(NB)]
            dmC = sb("c_dm", [128, 9 * 128], F32, ph)
            tmp = [sb("c_t%d" % i, [128, 3, 512], F32, ph) for i in range(2)]
            pT = [sb("c_p%d" % i, [128, 3, 512], BF16, ph) for i in range(2)]
            dn = [sb("c_dn%d" % i, [128, 512], F32, ph) for i in range(2)]
            skr = sb("c_sk", [1, 16], F32, ph)
            esk = sb("c_esk", [128, 16], F32, ph)
            lsl = [kk.slot() for _ in range(NB)]
            osl = [kk.slot() for _ in range(NB)]
            dsl = kk.slot()
            dsl.dma('sp', dmC[:], dmC_d[:, :])
            t_dm = dsl.dma('sp', skr[:], sink_d[:, :])
            kk.wait('pe', t_dm)
            tsk = kk.mark('pe', pe.matmul(pf[5][:, 0:16], lhsT=ones_f[0:1, :], rhs=skr[0:1, :], start=True, stop=True))
            kk.wait('act', tsk)
            t_esk = kk.mark('act', act.activation(out=esk[:], in_=pf[5][:, 0:16], func=AF.Exp))
            pf_free[5] = t_esk
            hfree = [None] * NB
            ltok = {}
            QT1v = QT1.rearrange("(h p) t -> p h t", p=128)
            OT1v = OT1.rearrange("(h p) t -> p h t", p=128)

            def issue_grp(g):
                s = g % NB
                kk.wait('sp', hfree[s])
                lsl[s].dma('sp', qT[s][:, :, :], QT1v[:, 4 * g:4 * g + 4, :])
                lsl[s].dma('sp', kT[s][:], KT1[g * 128:(g + 1) * 128, :])
                ltok[g] = lsl[s].dma('sp', vT[s][:], VT1[g * 128:(g + 1) * 128, :])

            issue_grp(0)
            tmp_free = [None, None]
            pT_free = [None, None]
            dn_free = [None, None]
            te_tok = {}
            gidx = [0]
            for g in range(4):
                if g + 1 < 4:
                    issue_grp(g + 1)
                s = g % NB
                kk.wait('pe', ltok[g])
                kk.wait('dve', ltok[g], t_dm, t_esk, osl[s].tok())
                vt = v_transposes(vT[s], [(vT[s][:, t * 128:(t + 1) * 128], 128, t) for t in range(18)], vv[s])
                kk.wait('pe', *vt)
                last_o = [None]

                def stage1(b, gi):
                    var = 0 if b == 0 else (2 if b == 15 else 1)
                    u = gi % 2
                    sb0 = 3 * u
                    mm = None
                    for c in range(3):
                        kk.wait('pe', pf_free[sb0 + c])
                        mm = pe.matmul(pf[sb0 + c][:, :], lhsT=kT[s][:, 128 * (b + c):128 * (b + c + 1)], rhs=qT[s][:, :, 128 * b:128 * (b + 1)],
                                       start=True, stop=True)
                    mm = kk.mark('pe', mm)
                    kk.wait('dve', mm, tmp_free[u])
                    ta = None
                    for c in range(3):
                        for hh in range(4):
                            slope = 2.0 ** (-(4 * g + hh + 1) / 2.0)
                            doff = (var * 3 + c) * 128
                            ta = dve.scalar_tensor_tensor(out=tmp[u][:, c, hh * 128:(hh + 1) * 128], in0=dmC[:, doff:doff + 128],
                                                          scalar=-slope, in1=pf[sb0 + c][:, hh * 128:(hh + 1) * 128], op0=ALU.mult, op1=ALU.add)
                        ta = kk.mark('dve', ta)
                        pf_free[sb0 + c] = ta
                    kk.wait('act', ta, pT_free[u])
                    te = kk.mark('act', act.activation(out=pT[u][:, :, :], in_=tmp[u][:, :, :], func=AF.Exp))
                    tmp_free[u] = te
                    te_tok[gi] = te

                def stage2(b, gi):
                    u = gi % 2
                    kk.wait('pe', te_tok.pop(gi), pf_free[6], pf_free[7])
                    for c in range(3):
                        pe.matmul(pf[6][:, :], lhsT=vv[s][:, b + c, :], rhs=pT[u][:, c, :], start=(c == 0), stop=(c == 2))
                    mm2 = None
                    for c in range(3):
                        mm2 = pe.matmul(pf[7][:, :], lhsT=ones_b[:, :], rhs=pT[u][:, c, :], start=(c == 0), stop=(c == 2))
                    mm2 = kk.mark('pe', mm2)
                    pT_free[u] = mm2
                    kk.wait('dve', mm2, dn_free[u])
                    for hh in range(4):
                        hi = 4 * g + hh
                        dve.tensor_scalar(out=dn[u][:, hh * 128:(hh + 1) * 128], in0=pf[7][:, hh * 128:(hh + 1) * 128],
                                          scalar1=esk[:, hi:hi + 1], scalar2=None, op0=ALU.add)
                    tr = kk.mark('dve', dve.reciprocal(out=dn[u][:], in_=dn[u][:]))
                    pf_free[7] = tr
                    kk.wait('dve', tr)
                    lo = kk.mark('dve', dve.tensor_tensor(out=oT[s][:, :, 128 * b:128 * (b + 1)],
                                                          in0=pf[6][:, :].rearrange("p (h t) -> p h t", h=4),
                                                          in1=dn[u][:, :].rearrange("p (h t) -> p h t", h=4), op=ALU.mult))
                    dn_free[u] = lo
                    pf_free[6] = lo
                    last_o[0] = lo

                LAG = 1
                g0 = gidx[0]
                for i in range(16 + LAG):
                    if i < 16:
                        stage1(i, g0 + i)
                    if i - LAG >= 0:
                        stage2(i - LAG, g0 + i - LAG)
                gidx[0] += 16
                kk.wait('sp', last_o[0])
                osl[s].dma('sp', OT1v[:, 4 * g:4 * g + 4, :], oT[s][:, :, :])
                hfree[s] = last_o[0]
            barrier()


        return pre_w1

    if MODE == 'mlp_only':
        mlp(0, x_ext[QOFF + 128:QOFF + 128 + OWN, :], OWN, y_out, False, "m0")
    elif MODE == 'l0_only':
        pre0 = layer0_front()
        out_proj(w_out0, OT0, NQ0, lambda i: x_ext[QOFF + i * 128:QOFF + (i + 1) * 128, :], X1a, "op0", pre0)
        mlp(0, X1a[128:128 + OWN, :], OWN, y_out, False, "m0")
    else:
        pre0 = layer0_front()
        out_proj(w_out0, OT0, NQ0, lambda i: x_ext[QOFF + i * 128:QOFF + (i + 1) * 128, :], X1a, "op0", pre0)
        mlp(0, X1a, NQ0, X1, False, "m0")
        pre1 = layer1_front()
        out_proj(w_out1, OT1, OWN, lambda i: X1[128 + i * 128:128 + (i + 1) * 128, :], X2a, "op1", pre1)
        mlp(1, X2a, OWN, y_out, True, "m1")
    es.close()
    return nc


_CACHE = {}


def kernel(x, attn_norm, mlp_norm, w_mlp_in, w_mlp_out, even_w_in, even_rpb, even_w_out,
           odd_w_qkv, odd_sink, odd_w_out, final_norm):
    x = np.asarray(x, np.float32)
    f = lambda a: np.ascontiguousarray(np.asarray(a, np.float32))
    if 'nc' not in _CACHE:
        _CACHE['nc'] = build_program()
    nc = _CACHE['nc']
    gains = np.stack([f(attn_norm)[0], f(attn_norm)[1], f(mlp_norm)[0], f(mlp_norm)[1], f(final_norm)], 0)
    rpb = f(even_rpb)[0]
    shared = {
        "gains": np.ascontiguousarray(gains),
        "w_in0": f(even_w_in)[0], "w_out0": f(even_w_out)[0],
        "w_qkv1": f(odd_w_qkv)[0], "w_out1": f(odd_w_out)[0],
        "w1_0": f(w_mlp_in)[0], "w1_1": f(w_mlp_in)[1],
        "w2_0": f(w_mlp_out)[0], "w2_1": f(w_mlp_out)[1],
        "sink": f(odd_sink).reshape(1, 16),
        "ident": np.eye(128, dtype=np.float32),
    }
    in_maps = []
    for c in range(NCORE):
        biasA, dmB, dmC = host_tables(c, rpb)
        m = dict(shared)
        m["x_ext"] = host_x_ext(c, x)
        m["biasA"] = np.ascontiguousarray(biasA.reshape(8, 128, 25 * 128))
        m["dmB"] = np.ascontiguousarray(dmB.reshape(128, NVB * 2 * 128))
        m["dmC"] = np.ascontiguousarray(dmC.reshape(128, 9 * 128))
        in_maps.append(m)
    res = run_bass_kernel_spmd(nc, in_maps, core_ids=list(range(NCORE)))
    if DEBUG:
        _CACHE['res'] = res
    out = np.zeros((2, SEQ, D), np.float32)
    for c in range(NCORE):
        out[c // 4, (c % 4) * OWN:(c % 4 + 1) * OWN] = res.results[c]["y"]
    return out
```
